# Optimizing a Trainium2 kernel written in Bass

```python
import jax, jax.numpy as jnp
from jax import lax
import numpy as np


D_MODEL = 1024
BATCH = 4
SEQ = 8192
DEPTH = 2

GRID_W = 64
CTX_LEN = 256
N_EVEN = (DEPTH + 1) // 2
N_ODD = DEPTH // 2
RMS_EPS = 1e-6

POOL_WIDTH = D_MODEL // 2
POOL_WINDOWS = (2, 4, 8, 16)
POOL_GROUP = POOL_WIDTH // len(POOL_WINDOWS)
HG_WIDTH = D_MODEL // 2
HG_HEAD_DIM = 128
HG_HEADS = HG_WIDTH // HG_HEAD_DIM
HG_CHUNK = 64
AB_IN = POOL_WIDTH + 5 * HG_WIDTH
AB_MIX = POOL_WIDTH + HG_WIDTH
ATT_HEAD_DIM = 128
ATT_HEADS = D_MODEL // ATT_HEAD_DIM
ATT_KV_HEADS = 2
ATT_GROUP = ATT_HEADS // ATT_KV_HEADS
ATT_IN = (ATT_HEADS + 2 * ATT_KV_HEADS) * ATT_HEAD_DIM
Q_BLOCK = 128
ROPE_THETA = 10000.0
FFN_HIDDEN = ((8 * D_MODEL + 3 * 256 - 1) // (3 * 256)) * 256

kernel_name = "hybrid_pool_hgrn2_gqa_diffusion_trunk"

F32 = jnp.float32


def rmsnorm(x, g):
    xf = x.astype(F32)
    y = xf * lax.rsqrt(jnp.mean(xf * xf, axis=-1, keepdims=True) + RMS_EPS)
    return (y * g.astype(F32)).astype(x.dtype)


def flip(a):
    return a[:, ::-1]


def swiglu(h, w_in, w_out):
    a, b = jnp.split(h @ w_in, 2, axis=-1)
    return (jax.nn.silu(a) * b) @ w_out


def grid_rope(x):
    L = x.shape[1]
    rows = L // GRID_W
    row = jnp.repeat(jnp.arange(rows, dtype=F32), GRID_W)
    col = jnp.tile(jnp.arange(GRID_W, dtype=F32), rows)
    n_freq = ATT_HEAD_DIM // 4
    inv = ROPE_THETA ** (-jnp.arange(n_freq, dtype=F32) / n_freq)

    def rot(xa, pos):
        ang = pos[:, None] * inv[None, :]
        cos = jnp.cos(ang)[:, None, :]
        sin = jnp.sin(ang)[:, None, :]
        x1, x2 = jnp.split(xa, 2, axis=-1)
        return jnp.concatenate([x1 * cos - x2 * sin, x2 * cos + x1 * sin], axis=-1)

    xf = x.astype(F32)
    half = ATT_HEAD_DIM // 2
    return jnp.concatenate([rot(xf[..., :half], row), rot(xf[..., half:], col)], axis=-1).astype(x.dtype)


def multiscale_pool(u, pool_w, pool_scale):
    Bn, Ln, _ = u.shape
    cs = jnp.pad(jnp.cumsum(u.astype(F32), axis=1), ((0, 0), (1, 0), (0, 0)))
    t = jnp.arange(Ln)
    groups = []
    for gi, w in enumerate(POOL_WINDOWS):
        lo = jnp.clip(t - w // 2, 0, Ln)
        hi = jnp.clip(t - w // 2 + w, 0, Ln)
        sl = slice(gi * POOL_GROUP, (gi + 1) * POOL_GROUP)
        csg = cs[..., sl]
        mean = (csg[:, hi] - csg[:, lo]) / (hi - lo).astype(F32)[None, :, None]
        groups.append(mean - u[..., sl].astype(F32))
    y = jnp.stack(groups, axis=2).astype(u.dtype)
    y = jnp.einsum('blgc,gcd->blgd', y, pool_w).reshape(Bn, Ln, POOL_WIDTH)
    return y * pool_scale


def hgrn_scan(q, k, v, logf, s0):
    Bn, Ln, H, _ = q.shape
    dv = v.shape[-1]
    n = Ln // HG_CHUNK

    def chunks(a):
        return jnp.moveaxis(a.reshape(Bn, n, HG_CHUNK, H, a.shape[-1]), 1, 0)

    causal = jnp.tril(jnp.ones((HG_CHUNK, HG_CHUNK), dtype=bool))[None, :, :, None, None]

    def step(S, inp):
        qc, kc, vc, gc = inp
        b = jnp.cumsum(gc, axis=1)
        dec = jnp.exp(jnp.where(causal, b[:, :, None] - b[:, None, :], -jnp.inf))
        a = jnp.einsum('bthd,btshd->btsh', qc, dec * kc[:, None])
        o = jnp.einsum('btsh,bshe->bthe', a, vc) + jnp.einsum('bthd,bhde->bthe', qc * jnp.exp(b), S)
        b_last = b[:, -1]
        S = jnp.exp(b_last)[..., None] * S + jnp.einsum('bshd,bshe->bhde', kc * jnp.exp(b_last[:, None] - b), vc)
        return S, o

    s_fin, o = lax.scan(step, s0, (chunks(q), chunks(k), chunks(v), chunks(logf)))
    return jnp.moveaxis(o, 0, 1).reshape(Bn, Ln, H, dv), s_fin


def pool_hgrn_mixer(h_lat, h_ctx, w_in, w_out, pool_w, pool_scale, lb, onorm_g, need_ctx):
    def project(h):
        z = h @ w_in
        Bn, Ln = z.shape[:2]
        u = z[..., :POOL_WIDTH]
        q, zf, zb, v, g = jnp.split(z[..., POOL_WIDTH:], 5, axis=-1)
        heads = lambda a: a.reshape(Bn, Ln, HG_HEADS, HG_HEAD_DIM).astype(F32)
        return u, heads(jax.nn.silu(q)), heads(zf), heads(zb), heads(v), g

    def gate(z, lb_dir):
        f = lb_dir + (1.0 - lb_dir) * jax.nn.sigmoid(z)
        return 1.0 - f, jnp.log(f)

    lb_f = lb[0].reshape(HG_HEADS, HG_HEAD_DIM)
    lb_b = lb[1].reshape(HG_HEADS, HG_HEAD_DIM)
    u_c, q_c, zf_c, zb_c, v_c, g_c = project(h_ctx)
    u_l, q_l, zf_l, zb_l, v_l, g_l = project(h_lat)
    k_cf, lf_cf = gate(zf_c, lb_f)
    k_cb, lf_cb = gate(zb_c, lb_b)
    k_lf, lf_lf = gate(zf_l, lb_f)
    k_lb, lf_lb = gate(zb_l, lb_b)

    s0 = jnp.zeros((h_ctx.shape[0], HG_HEADS, HG_HEAD_DIM, HG_HEAD_DIM), F32)
    o_cf, s_f = hgrn_scan(q_c, k_cf, v_c, lf_cf, s0)
    o_cb, s_b = hgrn_scan(flip(q_c), flip(k_cb), flip(v_c), flip(lf_cb), s0)
    o_lf, _ = hgrn_scan(q_l, k_lf, v_l, lf_lf, s_f)
    o_lb, _ = hgrn_scan(flip(q_l), flip(k_lb), flip(v_l), flip(lf_lb), s_b)

    def readout(o_sum, g, u):
        Bn, Ln = g.shape[:2]
        o = rmsnorm(o_sum, onorm_g).reshape(Bn, Ln, HG_WIDTH).astype(g.dtype) * jax.nn.silu(g)
        return jnp.concatenate([multiscale_pool(u, pool_w, pool_scale), o], axis=-1) @ w_out

    y_lat = readout(o_lf + flip(o_lb), g_l, u_l)
    y_ctx = readout(o_cf + flip(o_cb), g_c, u_c) if need_ctx else None
    return y_lat, y_ctx


def gqa_mixer(h_lat, h_ctx, w_in, w_out, qn_g, kn_g, need_ctx):
    dh = ATT_HEAD_DIM

    def project(h):
        z = h @ w_in
        Bn, Ln = z.shape[:2]
        q = z[..., :ATT_HEADS * dh].reshape(Bn, Ln, ATT_HEADS, dh)
        k = z[..., ATT_HEADS * dh:(ATT_HEADS + ATT_KV_HEADS) * dh].reshape(Bn, Ln, ATT_KV_HEADS, dh)
        v = z[..., (ATT_HEADS + ATT_KV_HEADS) * dh:].reshape(Bn, Ln, ATT_KV_HEADS, dh)
        return rmsnorm(q, qn_g), rmsnorm(k, kn_g), v

    Bn, L, _ = h_lat.shape
    Lc = h_ctx.shape[1]
    q_l, k_l, v_l = project(h_lat)
    q_c, k_c, v_c = project(h_ctx)
    q_l = grid_rope(q_l)
    k_l = grid_rope(k_l)
    k_all = jnp.concatenate([k_l, k_c], axis=1)
    v_all = jnp.concatenate([v_l, v_c], axis=1)
    scale = dh ** -0.5

    def attend(qb, k, v):
        s = jnp.einsum('bqhgd,bkhd->bhgqk', qb, k, preferred_element_type=F32) * scale
        p = jax.nn.softmax(s, axis=-1).astype(v.dtype)
        return jnp.einsum('bhgqk,bkhd->bqhgd', p, v)

    q_blocks = jnp.moveaxis(q_l.reshape(Bn, L // Q_BLOCK, Q_BLOCK, ATT_KV_HEADS, ATT_GROUP, dh), 1, 0)
    o = lax.map(lambda qb: attend(qb, k_all, v_all), q_blocks)
    y_lat = jnp.moveaxis(o, 0, 1).reshape(Bn, L, ATT_HEADS * dh) @ w_out
    y_ctx = None
    if need_ctx:
        o_c = attend(q_c.reshape(Bn, Lc, ATT_KV_HEADS, ATT_GROUP, dh), k_c, v_c)
        y_ctx = o_c.reshape(Bn, Lc, ATT_HEADS * dh) @ w_out
    return y_lat, y_ctx


def setup_inputs(seed: int = 0) -> dict:
    key = jax.random.key(seed)
    ks = jax.random.split(key, 20)
    D = D_MODEL
    nrm = lambda k, shape, s: jax.random.normal(k, shape, F32) * s
    return {
        "x": nrm(ks[0], (BATCH, SEQ, D), 1.0),
        "c": nrm(ks[1], (BATCH, D), 1.0),
        "ctx": nrm(ks[2], (BATCH, CTX_LEN, D), 1.0),
        "c_ctx": nrm(ks[3], (D,), 1.0),
        "ada_w": nrm(ks[4], (DEPTH, D, 6 * D), 0.5 * D ** -0.5),
        "ada_b": nrm(ks[5], (DEPTH, 6 * D), 0.02),
        "norm_g": 1.0 + nrm(ks[6], (DEPTH, 4, D), 0.02),
        "ab_w_in": nrm(ks[7], (N_EVEN, D, AB_IN), D ** -0.5),
        "ab_w_out": nrm(ks[8], (N_EVEN, AB_MIX, D), AB_MIX ** -0.5),
        "pool_w": nrm(ks[9], (N_EVEN, len(POOL_WINDOWS), POOL_GROUP, POOL_GROUP), POOL_GROUP ** -0.5),
        "pool_scale": 1.0 + nrm(ks[10], (N_EVEN, POOL_WIDTH), 0.1),
        "hg_lower": nrm(ks[11], (N_EVEN + 1, 2, HG_WIDTH), 0.1),
        "hg_onorm_g": 1.0 + nrm(ks[12], (N_EVEN, HG_HEAD_DIM), 0.02),
        "att_w_in": nrm(ks[13], (N_ODD, D, ATT_IN), D ** -0.5),
        "att_w_out": nrm(ks[14], (N_ODD, ATT_HEADS * ATT_HEAD_DIM, D), (ATT_HEADS * ATT_HEAD_DIM) ** -0.5),
        "att_qnorm_g": 1.0 + nrm(ks[15], (N_ODD, ATT_HEAD_DIM), 0.02),
        "att_knorm_g": 1.0 + nrm(ks[16], (N_ODD, ATT_HEAD_DIM), 0.02),
        "ffn_w_in": nrm(ks[17], (DEPTH, D, 2 * FFN_HIDDEN), D ** -0.5),
        "ffn_w_out": nrm(ks[18], (DEPTH, FFN_HIDDEN, D), FFN_HIDDEN ** -0.5),
    }


def reference(x, c, ctx, c_ctx, ada_w, ada_b, norm_g, ab_w_in, ab_w_out, pool_w, pool_scale,
              hg_lower, hg_onorm_g, att_w_in, att_w_out, att_qnorm_g, att_knorm_g, ffn_w_in, ffn_w_out):
    lb_all = jnp.cumsum(jax.nn.softmax(hg_lower.astype(F32), axis=0), axis=0)
    ctx_s = ctx
    for l in range(DEPTH):
        j = l // 2
        need_ctx = l < DEPTH - 1
        m_lat = (jax.nn.silu(c) @ ada_w[l] + ada_b[l])[:, None]
        m_ctx = jax.nn.silu(c_ctx) @ ada_w[l] + ada_b[l]
        sh1, sc1, g1, sh2, sc2, g2 = jnp.split(m_lat, 6, axis=-1)
        csh1, csc1, cg1, csh2, csc2, cg2 = jnp.split(m_ctx, 6, axis=-1)

        h_lat = rmsnorm(x, norm_g[l, 0]) * (1.0 + sc1) + sh1
        h_ctx = rmsnorm(ctx_s, norm_g[l, 0]) * (1.0 + csc1) + csh1
        if l % 2 == 0:
            y_lat, y_ctx = pool_hgrn_mixer(h_lat, h_ctx, ab_w_in[j], ab_w_out[j], pool_w[j], pool_scale[j],
                                           lb_all[j], hg_onorm_g[j], need_ctx)
        else:
            y_lat, y_ctx = gqa_mixer(h_lat, h_ctx, att_w_in[j], att_w_out[j], att_qnorm_g[j], att_knorm_g[j],
                                     need_ctx)
        x = x + g1 * rmsnorm(y_lat, norm_g[l, 1])
        f_lat = swiglu(rmsnorm(x, norm_g[l, 2]) * (1.0 + sc2) + sh2, ffn_w_in[l], ffn_w_out[l])
        x = x + g2 * rmsnorm(f_lat, norm_g[l, 3])

        if need_ctx:
            ctx_s = ctx_s + cg1 * rmsnorm(y_ctx, norm_g[l, 1])
            f_ctx = swiglu(rmsnorm(ctx_s, norm_g[l, 2]) * (1.0 + csc2) + csh2, ffn_w_in[l], ffn_w_out[l])
            ctx_s = ctx_s + cg2 * rmsnorm(f_ctx, norm_g[l, 3])
    return x
```

```python
import contextlib
import numpy as np
import ml_dtypes
import concourse.bass as bass
import concourse.mybir as mybir
from concourse.bass_utils import run_bass_kernel_spmd

ACT = mybir.ActivationFunctionType
ALU = mybir.AluOpType
F32 = mybir.dt.float32
BF16 = mybir.dt.bfloat16

D = 1024
KC = 8
SEQ = 8192
CTX = 256
LT = SEQ + CTX
OWN = 4096
FH = 2816
HC = 22
EPS = 1e-6
NCORES = 8


class Tok:
    __slots__ = ("sem", "val")

    def __init__(self, sem, val):
        self.sem = sem
        self.val = val


class Buf:
    def __init__(self, name=""):
        self.name = name
        self.w = None
        self.r = {}


class Eng:
    def __init__(self, name):
        self.name = name
        self.ops = []
        self.count = 0
        self.sem = None
        self.seen = {}


class Prog:
    ENGS = ("pe", "act", "dve", "pool", "sp")

    def __init__(self, nc, n_slots=None):
        self.nc = nc
        self.stack = contextlib.ExitStack()
        self.engs = {n: Eng(n) for n in self.ENGS}
        for n, e in self.engs.items():
            e.sem = self.stack.enter_context(nc.semaphore("s_" + n))
        n_slots = n_slots or {"sp": 8, "pool": 6}
        self.slots = {}
        self.slot_rr = {}
        for q, k in n_slots.items():
            self.slots[q] = [[self.stack.enter_context(nc.semaphore(f"d_{q}{i}")), 0] for i in range(k)]
            self.slot_rr[q] = 0

    def sbuf(self, name, shape, dtype):
        return self.stack.enter_context(self.nc.sbuf_tensor(name, list(shape), dtype))

    def psum(self, name, shape, dtype):
        return self.stack.enter_context(self.nc.psum_tensor(name, list(shape), dtype))

    def _collect(self, e, reads, writes):
        waits = {}

        def need(tok):
            if tok is None:
                return
            if tok.sem is e.sem and e.name == "pe":
                return
            k = id(tok.sem)
            if k not in waits or waits[k].val < tok.val:
                waits[k] = tok

        for b in reads:
            need(b.w)
        for b in writes:
            need(b.w)
            for t in b.r.values():
                need(t)
        return waits

    def _emit_waits(self, e, waits):
        for k, tok in waits.items():
            if e.seen.get(k, 0) >= tok.val:
                continue
            e.seen[k] = tok.val
            e.ops.append(("wait", tok.sem, tok.val))

    def _mark(self, tok, reads, writes):
        k = id(tok.sem)
        for b in reads:
            b.r[k] = tok
        for b in writes:
            b.w = tok
            b.r = {}

    def op(self, eng, fn, reads=(), writes=()):
        e = self.engs[eng]
        self._emit_waits(e, self._collect(e, reads, writes))
        e.count += 1
        tok = Tok(e.sem, e.count)
        e.ops.append(("op", fn, e.sem, 1))
        self._mark(tok, reads, writes)
        return tok

    def dma(self, q, out, in_, reads=(), writes=()):
        e = self.engs[q]
        i = self.slot_rr[q]
        self.slot_rr[q] = (i + 1) % len(self.slots[q])
        slot = self.slots[q][i]
        waits = self._collect(e, reads, writes)
        if slot[1] > 0:
            k = id(slot[0])
            prev = Tok(slot[0], 16 * slot[1])
            if k not in waits or waits[k].val < prev.val:
                waits[k] = prev
        self._emit_waits(e, waits)
        slot[1] += 1
        tok = Tok(slot[0], 16 * slot[1])
        e.ops.append(("op", lambda h, o=out, i_=in_: h.dma_start(out=o, in_=i_), slot[0], 16))
        self._mark(tok, reads, writes)
        return tok

    def barrier(self):
        toks = []
        for e in self.engs.values():
            if e.count:
                toks.append(Tok(e.sem, e.count))
        for q, sl in self.slots.items():
            for s in sl:
                if s[1]:
                    toks.append(Tok(s[0], 16 * s[1]))
        for e in self.engs.values():
            w = {}
            for t in toks:
                if t.sem is e.sem:
                    continue
                w[id(t.sem)] = t
            self._emit_waits(e, w)

    def finish(self):
        self.barrier()
        with self.nc.Block() as block:
            def mk(name):
                def body(h):
                    for o in self.engs[name].ops:
                        if o[0] == "wait":
                            h.wait_ge(o[1], o[2])
                        else:
                            o[1](h).then_inc(o[2], o[3])
                return body
            block.tensor(mk("pe"))
            block.scalar(mk("act"))
            block.vector(mk("dve"))
            block.gpsimd(mk("pool"))
            block.sync(mk("sp"))
        self.stack.close()


ARENA = 53184


class Builder:
    def __init__(self, dbg=None, stop_after=None):
        self.dbg = dbg
        self.stop_after = stop_after
        nc = bass.Bass("TRN2", target_bir_lowering=False)
        self.nc = nc
        di = lambda n, s, dt=F32: nc.dram_tensor(n, list(s), dt, kind="ExternalInput").ap()
        self.x_in = di("x_loc", [SEQ, D])
        self.ctx_in = di("ctx_loc", [CTX, D])
        self.vecs1 = di("vecs1", [112, 128])
        self.vecs2 = di("vecs2", [87, 128])
        self.ada_w = di("ada_w", [2, D, 6 * D])
        self.w_in0 = di("w_in0", [D, 3072])
        self.w_out0 = di("w_out0", [D, D])
        self.pool_w = di("pool_w", [4, 128, 128])
        self.att_w_in = di("att_w_in", [D, 1536])
        self.att_w_out = di("att_w_out", [D, D])
        self.ffn_w_in = di("ffn_w_in", [2, D, 2 * FH])
        self.ffn_w_out = di("ffn_w_out", [2, FH, D])
        self.pm_in = di("pmats", [128, 36 * 128])
        self.invc_in = di("invcnt", [128, 12 * 128])
        self.cos_in = di("cosT", [128, SEQ])
        self.sin_in = di("sinT", [128, SEQ])
        self.cst_in = di("consts", [128, 128 * 2 + 64 * 2 + 512])
        self.out = nc.dram_tensor("out_loc", [OWN, D], F32, kind="ExternalOutput").ap()
        ds = lambda n, s, dt: nc.dram_tensor(n, list(s), dt).ap()
        self.xT_d = ds("xT_d", [KC, 128, LT], F32)
        self.qs_d = ds("qs_d", [4, 128, LT], BF16)
        self.z_d = ds("z_d", [8, 128, LT], F32)
        self.gs_d = ds("gs_d", [4, 128, LT], BF16)
        self.u_d = ds("u_d", [LT, 512], BF16)
        self.v_d = ds("v_d", [LT, 512], BF16)
        self.o1_d = ds("o1_d", [4, 128, LT], F32)
        self.q1_d = ds("q1_d", [8, 128, OWN], BF16)
        self.b_xTg = [Buf(f"xT{i}") for i in range(LT // 256)]; self.b_xT = None; self.b_qs = Buf(); self.b_z = Buf(); self.b_gs = Buf()
        self.b_u = Buf(); self.b_v = Buf(); self.b_o1 = Buf(); self.b_q1 = Buf()
        self.dbg_out = {}
        if dbg:
            for name in dbg:
                if name == "dv":
                    continue
                self.dbg_out[name] = nc.dram_tensor("dbg_" + name, [KC, 128, LT], F32, kind="ExternalOutput").ap()
        self.P = Prog(nc)
        self.arena = self.P.sbuf("arena", [128, ARENA], F32)
        self.ptr = 0
        self.psall = self.P.psum("psall", [128, 4096], F32)
        self.ps = [self.psall[:, i * 512:(i + 1) * 512] for i in range(8)]
        self.bps = [[Buf(f"ps{i}")] for i in range(8)]
        self.rr = 0

    def af(self, n):
        a = self.arena[:, self.ptr:self.ptr + n]
        self.ptr += n
        assert self.ptr <= ARENA, f"arena overflow {self.ptr}"
        return a

    def ab(self, n):
        assert n % 2 == 0
        return self.af(n // 2).bitcast(BF16)

    def phase_end(self, keep):
        self.P.barrier()
        self.ptr = keep

    def mm(self, out, lhsT, rhs, start, stop, reads, writes, **kw):
        self.P.op("pe", lambda h: h.matmul(out, lhsT=lhsT, rhs=rhs, start=start, stop=stop, **kw), reads=reads, writes=writes)

    def act(self, out, in_, func, reads, writes, scale=1.0, bias=0.0):
        self.P.op("act", lambda h: h.activation(out=out, in_=in_, func=func, scale=scale, bias=bias), reads=reads, writes=writes)

    def tt(self, eng, out, in0, in1, op, reads, writes):
        self.P.op(eng, lambda h: h.tensor_tensor(out=out, in0=in0, in1=in1, op=op), reads=reads, writes=writes)

    def ts(self, eng, out, in0, s1, s2, op0, op1, reads, writes):
        self.P.op(eng, lambda h: h.tensor_scalar(out=out, in0=in0, scalar1=s1, scalar2=s2, op0=op0, op1=op1), reads=reads, writes=writes)

    def stt(self, eng, out, in0, scalar, in1, op0, op1, reads, writes):
        self.P.op(eng, lambda h: h.scalar_tensor_tensor(out=out, in0=in0, scalar=scalar, in1=in1, op0=op0, op1=op1), reads=reads, writes=writes)

    def cp(self, eng, out, in_, reads, writes):
        if eng == "act":
            self.P.op("act", lambda h: h.copy(out=out, in_=in_), reads=reads, writes=writes)
        else:
            self.P.op(eng, lambda h: h.tensor_copy(out=out, in_=in_), reads=reads, writes=writes)

    def ld(self, out, in_, reads=(), writes=()):
        self.P.dma("sp", out, in_, reads=reads, writes=writes)

    def st(self, out, in_, reads=(), writes=()):
        self.P.dma("pool", out, in_, reads=reads, writes=writes)

    def bank(self, i):
        return self.ps[i], self.bps[i]

    def tailbank(self):
        i = (0, 1, 2, 3, 6, 7)[self.rr % 6]
        self.rr += 1
        return self.ps[i], self.bps[i]

    def nextbank(self, lo=0, hi=8):
        i = lo + self.rr % (hi - lo)
        self.rr += 1
        return self.ps[i], self.bps[i]

    def load_w(self, dst, src2d, kcs, ncols, stage, bstage, bdst, SC):
        i = 0
        for kc in range(kcs):
            for c in range(0, ncols, SC):
                w = min(SC, ncols - c)
                s, bs = stage[i % 2], bstage[i % 2]
                self.ld(s[:, :w], src2d[kc * 128:(kc + 1) * 128, c:c + w], writes=[bs])
                self.cp("pool" if i % 2 == 0 else "dve", dst[:, kc, c:c + w], s[:, :w], reads=[bs], writes=[bdst])
                i += 1

    def rstd_of(self, src3, bsrc, kcs, N, dim, sq, bsq, rstd, brstd, scale_in=None):
        self.act(sq[:, :kcs, :N], src3, ACT.Square, reads=bsrc, writes=[bsq])
        ps, bp = self.tailbank() if getattr(self, "in_tail", False) else self.nextbank(4, 8)
        for kc in range(kcs):
            self.mm(ps[:, :N], self.ones_b, sq[:, kc, :N], kc == 0, kc == kcs - 1, reads=[bsq, self.b_c], writes=bp)
        self.act(rstd[:, :N], ps[:, :N], ACT.Ln, reads=bp + [self.b_c], writes=[brstd], scale=1.0 / dim, bias=self.eps_t[:, 0:1])
        self.act(rstd[:, :N], rstd[:, :N], ACT.Exp, reads=[brstd], writes=[brstd], scale=-0.5)

    def build(self):
        P = self.P
        cst = self.af(896); self.b_c = Buf("consts")
        self.ld(cst, self.cst_in, writes=[self.b_c])
        self.ident_f = cst[:, 0:128]
        self.rotT = cst[:, 128:256]
        self.mask_f = [cst[0:64, 256:320], cst[0:64, 320:384]]
        self.cmask = cst[:, 384:896]
        self.ident_b = self.ab(128)
        self.ones_b = self.ab(128)
        self.eps_t = self.af(2)
        self.cp("dve", self.ident_b, self.ident_f, reads=[self.b_c], writes=[self.b_c])
        P.op("pool", lambda h: h.memset(self.ones_b, 1.0), writes=[self.b_c])
        P.op("pool", lambda h: h.memset(self.eps_t, EPS), writes=[self.b_c])
        self.vT1 = self.af(112); self.vT2 = self.af(88)
        self.dv = self.af(2 * 2 * 6 * 8)
        self.hv = self.af(24)
        self.b_v1 = Buf("vecs")
        self.phase_M()
        keep0 = self.ptr
        if self.dbg and "dv" in self.dbg:
            dvo = self.nc.dram_tensor("dbg_dvec", [128, 216], F32, kind="ExternalOutput").ap()
            self.P.dma("sp", dvo[:, 0:192], self.dv, reads=[self.b_v1])
            self.P.dma("sp", dvo[:, 192:216], self.hv, reads=[self.b_v1])
        if self.stop_after == "M":
            return self.finish()
        self.phase_A()
        self.phase_end(keep0)
        if self.stop_after == "A":
            return self.finish()
        self.phase_B()
        self.phase_end(keep0)
        if self.dbg and "mid0" in self.dbg:
            self.P.dma("sp", self.dbg_out["mid0"], self.xT_d, reads=self.b_xTg)
        if self.stop_after == "B":
            return self.finish()
        self.phase_FFN(0, [(SEQ, CTX)] + [(i * 512, 512) for i in range(SEQ // 512)], final=False)
        self.phase_end(keep0)
        if self.dbg and "out0" in self.dbg:
            self.P.dma("sp", self.dbg_out["out0"], self.xT_d, reads=self.b_xTg)
        if self.stop_after == "C":
            return self.finish()
        self.phase_DE()
        self.phase_end(keep0)
        if self.dbg and "mid1" in self.dbg:
            self.P.dma("sp", self.dbg_out["mid1"], self.xT_d, reads=self.b_xTg)
        if self.stop_after == "E":
            return self.finish()
        self.phase_FFN(1, [(i * 512, 512) for i in range(OWN // 512)], final=True)
        return self.finish()

    def finish(self):
        self.P.finish()
        return self.nc

    def dvec(self, l, s, part):
        o = ((l * 2 + s) * 6 + part) * 8
        return self.dv[:, o:o + 8]

    def phase_M(self):
        P = self.P
        p0 = self.ptr
        r1 = self.af(128); r2 = self.af(128); b_r = Buf()
        self.ld(r1[0:112, :], self.vecs1, writes=[b_r])
        self.ld(r2[0:87, :], self.vecs2, writes=[b_r])
        ps, bp = self.bank(0)
        P.op("pe", lambda h: h.transpose(ps[:, 0:112], r1[0:112, :], self.ident_f[0:112, 0:112]), reads=[b_r, self.b_c], writes=bp)
        P.op("pe", lambda h: h.transpose(ps[:, 128:215], r2[0:87, :], self.ident_f[0:87, 0:87]), reads=[b_r, self.b_c], writes=bp)
        self.cp("dve", self.vT1, ps[:, 0:112], reads=bp, writes=[self.b_v1])
        self.cp("dve", self.vT2[:, 0:87], ps[:, 128:215], reads=bp, writes=[self.b_v1])
        scr = self.af(16); b_scr = Buf()
        scr3 = scr.rearrange("p (k s) -> p k s", s=2)
        self.act(scr3[:, :, 0], self.vT1[:, 0:8], ACT.Silu, reads=[self.b_v1], writes=[b_scr])
        self.act(scr3[:, :, 1], self.vT1[:, 8:16], ACT.Silu, reads=[self.b_v1], writes=[b_scr])
        lb = self.hv[:, 0:8]; oml = self.hv[:, 8:16]; noml = self.hv[:, 16:24]
        b_hv = self.b_v1
        self.tt("dve", lb, self.vT2[:, 68:76], self.vT2[:, 76:84], ALU.subtract, reads=[self.b_v1], writes=[b_hv])
        self.act(lb, lb, ACT.Sigmoid, reads=[b_hv], writes=[b_hv])
        self.ts("dve", oml, lb, -1.0, 1.0, ALU.mult, ALU.add, reads=[b_hv], writes=[b_hv])
        self.ts("dve", noml, oml, -1.0, 0.0, ALU.mult, ALU.add, reads=[b_hv], writes=[b_hv])
        aw = [self.af(6144), self.af(6144)]; b_aw = [Buf(), Buf()]
        mod = self.af(2 * 96); b_mod = Buf()
        i = 0
        macc = self.af(96); b_macc = Buf()
        for l in range(2):
            banks = [self.bank(1), self.bank(2)]
            for kc in range(KC):
                a, ba = aw[i % 2], b_aw[i % 2]
                psm, bpm = banks[kc // 4]
                self.ld(a, self.ada_w[l, kc * 128:(kc + 1) * 128, :], writes=[ba])
                for j in range(48):
                    o_ = (kc % 4) * 96 + 2 * j
                    self.mm(psm[:, o_:o_ + 2], a[:, j * 128:(j + 1) * 128], scr3[:, kc, :], True, True,
                            reads=[ba, b_scr], writes=bpm)
                i += 1
            self.cp("dve", macc, banks[0][0][:, 0:96], reads=banks[0][1], writes=[b_macc])
            self.tt("dve", macc, macc, banks[0][0][:, 96:192], ALU.add, reads=banks[0][1] + [b_macc], writes=[b_macc])
            self.tt("dve", macc, macc, banks[0][0][:, 192:288], ALU.add, reads=banks[0][1] + [b_macc], writes=[b_macc])
            self.tt("dve", macc, macc, banks[0][0][:, 288:384], ALU.add, reads=banks[0][1] + [b_macc], writes=[b_macc])
            for q_ in range(4):
                self.tt("dve", macc, macc, banks[1][0][:, q_ * 96:(q_ + 1) * 96], ALU.add, reads=banks[1][1] + [b_macc], writes=[b_macc])
            m3 = mod[:, l * 96:(l + 1) * 96].rearrange("p (j s) -> p j s", s=2)
            self.tt("dve", m3, macc.rearrange("p (j s) -> p j s", s=2),
                    self.vT1[:, 16 + l * 48:16 + (l + 1) * 48].unsqueeze(2).to_broadcast([128, 48, 2]), ALU.add,
                    reads=[b_macc, self.b_v1], writes=[b_mod])
            for s in range(2):
                part = lambda k: m3[:, k * 8:(k + 1) * 8, s]
                ng = lambda i_: self.vT2[:, (l * 4 + i_) * 8:(l * 4 + i_) * 8 + 8]
                rd = [b_mod, self.b_v1]
                self.stt("dve", self.dvec(l, s, 0), part(1), 1.0, ng(0), ALU.add, ALU.mult, reads=rd, writes=[self.b_v1])
                self.cp("dve", self.dvec(l, s, 1), part(0), reads=rd, writes=[self.b_v1])
                self.tt("dve", self.dvec(l, s, 2), part(2), ng(1), ALU.mult, reads=rd, writes=[self.b_v1])
                self.stt("dve", self.dvec(l, s, 3), part(4), 1.0, ng(2), ALU.add, ALU.mult, reads=rd, writes=[self.b_v1])
                self.cp("dve", self.dvec(l, s, 4), part(3), reads=rd, writes=[self.b_v1])
                self.tt("dve", self.dvec(l, s, 5), part(5), ng(3), ALU.mult, reads=rd, writes=[self.b_v1])
        self.P.barrier()
        self.ptr = p0

    def norm_mod(self, x3, bx, N, l, s, which, sq, bsq, rstd, brstd, tmp, btmp, hT, bhT):
        self.rstd_of(x3, [bx], KC, N, D, sq, bsq, rstd, brstd)
        self.tt("dve", tmp[:, :, :N], x3, rstd[:, :N].unsqueeze(1).to_broadcast([128, KC, N]), ALU.mult,
                reads=[bx, brstd], writes=[btmp])
        A = self.dvec(l, s, 0 if which == 1 else 3)
        B = self.dvec(l, s, 1 if which == 1 else 4)
        for kc in range(KC):
            if kc % 2 == 0:
                self.act(hT[:, kc, :N], tmp[:, kc, :N], ACT.Identity, reads=[btmp, self.b_v1], writes=[bhT],
                         scale=A[:, kc:kc + 1], bias=B[:, kc:kc + 1])
            else:
                self.ts("dve", hT[:, kc, :N], tmp[:, kc, :N], A[:, kc:kc + 1], B[:, kc:kc + 1], ALU.mult, ALU.add,
                        reads=[btmp, self.b_v1], writes=[bhT])

    def residual(self, y3, by, x3, bx, N, l, s, which, sq, bsq, rstd, brstd, tmp, btmp, out3, bout):
        self.rstd_of(y3, [by], KC, N, D, sq, bsq, rstd, brstd)
        self.tt("dve", tmp[:, :, :N], y3, rstd[:, :N].unsqueeze(1).to_broadcast([128, KC, N]), ALU.mult,
                reads=[by, brstd], writes=[btmp])
        G = self.dvec(l, s, 2 if which == 1 else 5)
        for kc in range(KC):
            self.stt("dve", out3[:, kc, :N], tmp[:, kc, :N], G[:, kc:kc + 1], x3[:, kc, :N], ALU.mult, ALU.add,
                     reads=[btmp, bx, self.b_v1], writes=[bout])

    def step_tail(self):
        if self.pending is not None:
            try:
                next(self.pending)
            except StopIteration:
                self.pending = None

    def step_G(self):
        if self.pendingG is not None:
            try:
                next(self.pendingG)
            except StopIteration:
                self.pendingG = None

    def flush_G(self):
        while self.pendingG is not None:
            self.step_G()

    def flush_tail(self):
        while self.pending is not None:
            self.step_tail()

    def xbufs(self, c0, N):
        return [self.b_xTg[i] for i in range(c0 // 256, (c0 + N) // 256)]

    def xT_src(self, c0, N):
        return self.xT_d[:, :, c0:c0 + N].rearrange("k p n -> p k n")

    def phase_A(self):
        P = self.P
        w = self.ab(KC * 3072).rearrange("p (k n) -> p k n", k=KC); b_w = Buf()
        stage = [self.af(3072), self.af(3072)]; b_stage = [Buf(), Buf()]
        self.load_w(w, self.w_in0, KC, 3072, stage, b_stage, b_w, 3072)
        xtok = [self.af(1024), self.af(1024)]; b_xtok = [Buf(), Buf()]
        xT2 = [self.af(KC * 512).rearrange("p (k n) -> p k n", k=KC) for _ in range(2)]; b_x2 = [Buf(), Buf()]
        sq = self.ab(KC * 512).rearrange("p (k n) -> p k n", k=KC); b_sq = Buf()
        rstd = self.af(512); b_rstd = Buf()
        tmp = self.af(KC * 512).rearrange("p (k n) -> p k n", k=KC); b_tmp = Buf()
        hT2 = [self.ab(KC * 512).rearrange("p (k n) -> p k n", k=KC) for _ in range(2)]; b_h2 = [Buf(), Buf()]
        qs = self.ab(4 * 512).rearrange("p (k n) -> p k n", k=4); b_qsst = Buf()
        gs = self.ab(4 * 512).rearrange("p (k n) -> p k n", k=4); b_gsst = Buf()
        zs = self.af(8 * 512).rearrange("p (k n) -> p k n", k=8); b_zst = Buf()
        uv = self.ab(4 * 1024).rearrange("p (t n) -> p t n", t=4); b_uv = Buf()
        blocks = [(SEQ, CTX, 1)] + [(i * 512, 512, 0) for i in range(SEQ // 512)]
        self.tcount = 0

        def stage_T(bi):
            c0, N, s = blocks[bi]
            NT = N // 128
            xT, b_x = xT2[bi % 2], b_x2[bi % 2]
            hT, b_h = hT2[bi % 2], b_h2[bi % 2]
            for ti in range(NT):
                xt, bxt = xtok[self.tcount % 2], b_xtok[self.tcount % 2]
                self.tcount += 1
                src = self.ctx_in[ti * 128:(ti + 1) * 128, :] if s else self.x_in[c0 + ti * 128:c0 + (ti + 1) * 128, :]
                self.ld(xt, src, writes=[bxt])
                for half in range(2):
                    ps, bp = self.nextbank(4, 8)
                    for k in range(4):
                        kc = half * 4 + k
                        P.op("pe", lambda h, o_=ps[:, k * 128:(k + 1) * 128], i_=xt[:, kc * 128:(kc + 1) * 128]: h.transpose(o_, i_, self.ident_f),
                             reads=[bxt, self.b_c], writes=bp)
                    self.cp("act" if half == 0 else "dve", xT[:, half * 4:(half + 1) * 4, ti * 128:(ti + 1) * 128],
                            ps[:, 0:512].rearrange("p (k n) -> p k n", k=4), reads=bp, writes=[b_x])
            self.st(self.xT_src(c0, N), xT[:, :, :N], reads=[b_x], writes=self.xbufs(c0, N))
            self.norm_mod(xT[:, :, :N], b_x, N, 0, s, 1, sq, b_sq, rstd, b_rstd, tmp, b_tmp, hT, b_h)

        def stage_P(bi):
            c0, N, s = blocks[bi]
            NT = N // 128
            hT, b_h = hT2[bi % 2], b_h2[bi % 2]
            for j in range(16):
                kind, hh = divmod(j, 4)
                col = [512, 1024, 1536, 2560][kind] + hh * 128
                ps, bp = self.nextbank(0, 4)
                for kc in range(KC):
                    self.mm(ps[:, :N], w[:, kc, col:col + 128], hT[:, kc, :N], kc == 0, kc == KC - 1, reads=[b_w, b_h], writes=bp)
                if kind == 0:
                    self.act(qs[:, hh, :N], ps[:, :N], ACT.Silu, reads=bp, writes=[b_qsst])
                elif kind == 3:
                    self.act(gs[:, hh, :N], ps[:, :N], ACT.Silu, reads=bp, writes=[b_gsst])
                else:
                    self.cp("dve", zs[:, (kind - 1) * 4 + hh, :N], ps[:, :N], reads=bp, writes=[b_zst])
            for ti in range(NT):
                for wi, col in enumerate((0, 2048)):
                    ps, bp = self.nextbank(0, 4)
                    for kc in range(KC):
                        self.mm(ps[:, :], hT[:, kc, ti * 128:(ti + 1) * 128], w[:, kc, col:col + 512], kc == 0, kc == KC - 1, reads=[b_w, b_h], writes=bp)
                    self.cp("act" if wi == 0 else "dve", uv[:, ti, wi * 512:(wi + 1) * 512], ps[:, :], reads=bp, writes=[b_uv])
            self.st(self.qs_d[:, :, c0:c0 + N].rearrange("k p n -> p k n"), qs[:, :, :N], reads=[b_qsst], writes=[self.b_qs])
            self.st(self.gs_d[:, :, c0:c0 + N].rearrange("k p n -> p k n"), gs[:, :, :N], reads=[b_gsst], writes=[self.b_gs])
            self.st(self.z_d[:, :, c0:c0 + N].rearrange("k p n -> p k n"), zs[:, :, :N], reads=[b_zst], writes=[self.b_z])
            self.st(self.u_d[c0:c0 + N, :].rearrange("(t p) e -> p t e", p=128), uv[:, :NT, 0:512], reads=[b_uv], writes=[self.b_u])
            self.st(self.v_d[c0:c0 + N, :].rearrange("(t p) e -> p t e", p=128), uv[:, :NT, 512:1024], reads=[b_uv], writes=[self.b_v])

        stage_T(0)
        for bi in range(len(blocks)):
            if bi + 1 < len(blocks):
                stage_T(bi + 1)
            stage_P(bi)

    def phase_B(self):
        P = self.P
        wo = self.ab(KC * D).rearrange("p (k n) -> p k n", k=KC); b_wo = Buf()
        pw = self.ab(4 * 128).rearrange("p (k n) -> p k n", k=4); b_pw = Buf()
        pm = self.ab(36 * 128).rearrange("p (k n) -> p k n", k=36); b_pm = Buf()
        invc = self.af(12 * 128).rearrange("p (k n) -> p k n", k=12); b_invc = Buf()
        p_stage = self.ptr
        stage = [self.af(4608), self.af(4608)]; b_stage = [Buf(), Buf()]
        self.load_w(wo, self.w_out0, KC, D, stage, b_stage, b_wo, D)
        for g in range(4):
            self.ld(stage[0][:, 0:128], self.pool_w[g], writes=[b_stage[0]])
            self.cp("dve", pw[:, g, :], stage[0][:, 0:128], reads=[b_stage[0]], writes=[b_pw])
        self.ld(stage[1][:, 0:4608], self.pm_in, writes=[b_stage[1]])
        self.cp("dve", pm, stage[1][:, 0:4608].rearrange("p (k n) -> p k n", k=36), reads=[b_stage[1]], writes=[b_pm])
        self.ld(invc, self.invc_in.rearrange("p (k n) -> p k n", k=12), writes=[b_invc])
        self.P.barrier()
        self.ptr = p_stage
        S2 = [self.af(512).rearrange("p (h e) -> p h e", h=4) for _ in range(2)]; b_S2 = [Buf(), Buf()]
        Sbf2 = [self.ab(512).rearrange("p (h e) -> p h e", h=4) for _ in range(2)]; b_Sbf2 = [Buf(), Buf()]
        qsb = self.ab(4 * 512).rearrange("p (k n) -> p k n", k=4); b_qsb = Buf()
        zb = self.af(4 * 512).rearrange("p (k n) -> p k n", k=4); b_zb = Buf()
        vc = self.ab(8 * 512).rearrange("p (c e) -> p c e", c=8); b_vc = Buf()
        sg = self.af(4 * 512).rearrange("p (k n) -> p k n", k=4); b_sg = Buf()
        lf = self.af(4 * 512).rearrange("p (k n) -> p k n", k=4); b_lf = Buf()
        kk = self.af(4 * 512).rearrange("p (k n) -> p k n", k=4); b_kk = Buf()
        bb = self.af(4 * 512).rearrange("p (k n) -> p k n", k=4); b_bb = Buf()
        qt2 = [self.ab(4 * 512).rearrange("p (k n) -> p k n", k=4) for _ in range(2)]; b_qt2 = [Buf(), Buf()]
        kt2 = [self.ab(4 * 512).rearrange("p (k n) -> p k n", k=4) for _ in range(2)]; b_kt2 = [Buf(), Buf()]
        ebl2 = [self.af(32).rearrange("p (h c) -> p h c", h=4) for _ in range(2)]; b_ebl2 = [Buf(), Buf()]
        ktok = [self.ab(512), self.ab(512)]; b_ktok = [Buf(), Buf()]
        Abf = [self.ab(256).rearrange("p (h t) -> p h t", h=4) for _ in range(3)]; b_Abf = [Buf(), Buf(), Buf()]
        ost2 = [self.af(4 * 512).rearrange("p (k n) -> p k n", k=4) for _ in range(2)]; b_ost2 = [Buf(), Buf()]
        gsb = self.ab(4 * 512).rearrange("p (k n) -> p k n", k=4); b_gsb = Buf()
        ub = self.ab(6 * 512).rearrange("p (t e) -> p t e", t=6); b_ub = Buf()
        xT = self.af(KC * 512).rearrange("p (k n) -> p k n", k=KC); b_x = Buf()
        sq = self.ab(KC * 512).rearrange("p (k n) -> p k n", k=KC); b_sq = Buf()

        mixT = self.ab(KC * 512).rearrange("p (k n) -> p k n", k=KC); b_mix = Buf()
        pooled = self.ab(4 * 512).rearrange("p (k n) -> p k n", k=4); b_pooled = Buf()
        yT = self.af(KC * 512).rearrange("p (k n) -> p k n", k=KC); b_y = Buf()
        rr4 = yT[:, 0:4, :]; b_rr4 = b_y
        rstd = self.af(512); b_rstd = Buf()
        lbv = self.hv[:, 0:8]; oml = self.hv[:, 8:16]; noml = self.hv[:, 16:24]

        for dirn in range(2):
            P.op("pool", lambda h: h.memset(S2[0], 0.0), writes=[b_S2[0]])
            self.sgen = 1
            P.op("pool", lambda h: h.memset(Sbf2[0], 0.0), writes=[b_Sbf2[0]])
            lat = [(i * 512, 512, 0) for i in range(SEQ // 512)]
            blocks = [(SEQ, CTX, 1)] + (lat if dirn == 0 else lat[::-1])
            def tail_gen(bi, c0, N, s, NT, seg0, segL, ost, b_ost):
                self.ld(gsb[:, :, :N], self.gs_d[:, :, c0:c0 + N].rearrange("k p n -> p k n"), reads=[self.b_gs], writes=[b_gsb])
                self.ld(xT[:, :, :N], self.xT_src(c0, N), reads=self.xbufs(c0, N), writes=[b_x])
                lo = max(c0 - 128, seg0); hi = min(c0 + N + 128, seg0 + segL)
                slot0 = 1 - (c0 - lo) // 128
                nld = (hi - lo) // 128
                self.ld(ub[:, slot0:slot0 + nld, :], self.u_d[lo:hi, :].rearrange("(t p) e -> p t e", p=128), reads=[self.b_u], writes=[b_ub])
                self.act(sq[:, 0:4, :N], ost[:, :, :N], ACT.Square, reads=[b_ost], writes=[b_sq])
                for hh in range(4):
                    ps, bp = self.tailbank()
                    self.mm(ps[:, :N], self.ones_b, sq[:, hh, :N], True, True, reads=[b_sq, self.b_c], writes=bp)
                    self.act(rr4[:, hh, :N], ps[:, :N], ACT.Ln, reads=bp + [self.b_c], writes=[b_rr4], scale=1.0 / 128, bias=self.eps_t[:, 0:1])
                self.act(rr4[:, :, :N], rr4[:, :, :N], ACT.Exp, reads=[b_rr4], writes=[b_rr4], scale=-0.5)
                self.tt("dve", ost[:, :, :N], ost[:, :, :N], rr4[:, :, :N], ALU.mult, reads=[b_ost, b_rr4], writes=[b_ost])
                self.stt("dve", mixT[:, 4:8, :N], ost[:, :, :N], self.vT2[:, 84:85], gsb[:, :, :N], ALU.mult, ALU.mult,
                         reads=[b_ost, b_gsb, self.b_v1], writes=[b_mix])
                yield
                ntiles_seg = segL // 128
                for g in range(4):
                    psp, bpp = self.tailbank()
                    ttps = []
                    for ti in range(NT):
                        gt = (c0 - seg0) // 128 + ti
                        ttp = 0 if gt == 0 else (2 if gt == ntiles_seg - 1 else 1)
                        ttps.append(ttp)
                        rels = [r for r in (-1, 0, 1) if 0 <= gt + r < ntiles_seg]
                        for ri, r in enumerate(rels):
                            self.mm(psp[:, ti * 128:(ti + 1) * 128], ub[:, 1 + ti + r, g * 128:(g + 1) * 128], pm[:, (ttp * 4 + g) * 3 + (r + 1), :],
                                    ri == 0, ri == len(rels) - 1, reads=[b_ub, b_pm], writes=bpp)
                    if all(t_ == 1 for t_ in ttps):
                        self.tt("dve", pooled[:, g, :N].rearrange("p (t n) -> p t n", t=NT), psp[:, :N].rearrange("p (t n) -> p t n", t=NT),
                                invc[:, 4 + g, :].unsqueeze(1).to_broadcast([128, NT, 128]), ALU.mult, reads=bpp + [b_invc], writes=[b_pooled])
                    else:
                        for ti in range(NT):
                            self.tt("dve", pooled[:, g, ti * 128:(ti + 1) * 128], psp[:, ti * 128:(ti + 1) * 128], invc[:, ttps[ti] * 4 + g, :], ALU.mult,
                                    reads=bpp + [b_invc], writes=[b_pooled])
                    psy, bpy = self.tailbank()
                    self.mm(psy[:, :N], pw[:, g, :], pooled[:, g, :N], True, True, reads=[b_pw, b_pooled], writes=bpy)
                    self.act(mixT[:, g, :N], psy[:, :N], ACT.Identity, reads=bpy + [self.b_v1], writes=[b_mix], scale=self.vT2[:, 64 + g:65 + g])
                    yield
                for oc in range(KC):
                    ps, bp = self.tailbank()
                    for kc in range(KC):
                        self.mm(ps[:, :N], wo[:, kc, oc * 128:(oc + 1) * 128], mixT[:, kc, :N], kc == 0, kc == KC - 1, reads=[b_wo, b_mix], writes=bp)
                    self.cp("act" if oc % 2 == 0 else "dve", yT[:, oc, :N], ps[:, :N], reads=bp, writes=[b_y])
                    if oc % 2 == 1:
                        yield
                self.in_tail = True
                self.residual(yT[:, :, :N], b_y, xT, b_x, N, 0, s, 1, sq, b_sq, rstd, b_rstd, yT, b_y, yT, b_y)
                self.in_tail = False
                self.st(self.xT_src(c0, N), yT[:, :, :N], reads=[b_y], writes=self.xbufs(c0, N))

            def G(bi):
                c0, N, s = blocks[bi]
                qt, b_qt = qt2[bi % 2], b_qt2[bi % 2]
                kt, b_kt = kt2[bi % 2], b_kt2[bi % 2]
                ebl, b_ebl = ebl2[bi % 2], b_ebl2[bi % 2]
                NCH = N // 64
                NT = N // 128
                seg0, segL = (SEQ, CTX) if s else (0, SEQ)
                self.ld(qsb[:, :, :N], self.qs_d[:, :, c0:c0 + N].rearrange("k p n -> p k n"), reads=[self.b_qs], writes=[b_qsb])
                self.ld(zb[:, :, :N], self.z_d[dirn * 4:(dirn + 1) * 4, :, c0:c0 + N].rearrange("k p n -> p k n"), reads=[self.b_z], writes=[b_zb])
                for hh in range(4):
                    self.act(sg[:, hh, :N], zb[:, hh, :N], ACT.Sigmoid, reads=[b_zb], writes=[b_sg])
                yield
                for hh in range(4):
                    i8 = dirn * 4 + hh
                    self.act(lf[:, hh, :N], sg[:, hh, :N], ACT.Ln, reads=[b_sg, self.b_v1], writes=[b_lf],
                             scale=oml[:, i8:i8 + 1], bias=lbv[:, i8:i8 + 1])
                    self.ts("pool", kk[:, hh, :N], sg[:, hh, :N], noml[:, i8:i8 + 1], oml[:, i8:i8 + 1], ALU.mult, ALU.add,
                            reads=[b_sg, self.b_v1], writes=[b_kk])
                yield
                for hh in range(4):
                    P.op("dve", lambda h, o_=bb[:, hh, :N], d0=self.cmask[:, :N], d1=lf[:, hh, :N]: h.tensor_tensor_scan(
                        out=o_, data0=d0, data1=d1, initial=0.0, op0=ALU.mult, op1=ALU.add),
                         reads=[b_lf, self.b_c], writes=[b_bb])
                    if dirn == 1:
                        b3 = bb[:, hh, :N].rearrange("p (c k) -> p c k", k=64)
                        l3 = lf[:, hh, :N].rearrange("p (c k) -> p c k", k=64)
                        t3 = sg[:, hh, :N].rearrange("p (c k) -> p c k", k=64)
                        self.tt("dve", t3, b3[:, :, 63:64].to_broadcast([128, NCH, 64]), b3, ALU.subtract, reads=[b_bb], writes=[b_sg])
                        self.tt("dve", b3, t3, l3, ALU.add, reads=[b_sg, b_lf], writes=[b_bb])
                yield
                for hh in range(4):
                    self.act(lf[:, hh, :N], bb[:, hh, :N], ACT.Exp, reads=[b_bb], writes=[b_lf])
                    self.act(sg[:, hh, :N], bb[:, hh, :N], ACT.Exp, reads=[b_bb], writes=[b_sg], scale=-1.0)
                yield
                self.tt("dve", qt[:, :, :N], qsb[:, :, :N], lf[:, :, :N], ALU.mult, reads=[b_qsb, b_lf], writes=[b_qt])
                self.tt("dve", kt[:, :, :N], kk[:, :, :N], sg[:, :, :N], ALU.mult, reads=[b_kk, b_sg], writes=[b_kt])
                epos = 63 if dirn == 0 else 0
                self.cp("pool", ebl[:, :, :NCH], lf[:, :, :N].rearrange("p h (c k) -> p h c k", k=64)[:, :, :, epos], reads=[b_lf], writes=[b_ebl])
            def C(bi):
                ost, b_ost = ost2[bi % 2], b_ost2[bi % 2]
                c0, N, s = blocks[bi]
                qt, b_qt = qt2[bi % 2], b_qt2[bi % 2]
                kt, b_kt = kt2[bi % 2], b_kt2[bi % 2]
                ebl, b_ebl = ebl2[bi % 2], b_ebl2[bi % 2]
                NCH = N // 64
                NT = N // 128
                seg0, segL = (SEQ, CTX) if s else (0, SEQ)
                self.ld(vc[0:64, :NCH, :], self.v_d[c0:c0 + N, :].rearrange("(c p) e -> p c e", p=64), reads=[self.b_v], writes=[b_vc])
                order = list(range(NCH)) if dirn == 0 else list(range(NCH))[::-1]
                if dirn == 1:
                    self.ld(ost[:, :, :N], self.o1_d[:, :, c0:c0 + N].rearrange("k p n -> p k n"), reads=[self.b_o1], writes=[b_ost])
                def stage1(ci):
                    c = order[ci]
                    cs = slice(c * 64, (c + 1) * 64)
                    d = ci % 2
                    a3 = ci % 3
                    psTb, bT = self.bank(0 + d)
                    psT = psTb[0:64, 0:256].bitcast(BF16)
                    psAb, bA = self.bank(2 + d)
                    psA = psAb[0:64, 0:256]
                    for hh in range(4):
                        P.op("pe", lambda h, o_=psT[:, hh * 128:(hh + 1) * 128], i_=kt[:, hh, cs]: h.transpose(o_, i_, self.ident_b),
                             reads=[b_kt, self.b_c], writes=bT)
                    self.cp("act", ktok[d][0:64, :], psT, reads=bT, writes=[b_ktok[d]])
                    for hh in range(4):
                        self.mm(psA[:, hh * 64:(hh + 1) * 64], kt[:, hh, cs], qt[:, hh, cs], True, True, reads=[b_kt, b_qt], writes=bA)
                    self.tt("dve", Abf[a3][0:64, :, :], psA.rearrange("p (h t) -> p h t", h=4),
                            self.mask_f[dirn].unsqueeze(1).to_broadcast([64, 4, 64]), ALU.mult, reads=bA + [self.b_c], writes=[b_Abf[a3]])

                def stage2(ci):
                    c = order[ci]
                    d = ci % 2
                    psKV, bKV = self.bank(4 + d)
                    for hh in range(4):
                        self.mm(psKV[:, hh * 128:(hh + 1) * 128], ktok[d][0:64, hh * 128:(hh + 1) * 128], vc[0:64, c, hh * 128:(hh + 1) * 128], True, True,
                                reads=[b_ktok[d], b_vc], writes=bKV)

                def update(ci):
                    c = order[ci]
                    d = ci % 2
                    g_ = self.sgen % 2
                    self.sgen += 1
                    psKV, bKV = self.bank(4 + d)
                    So, bSo = S2[1 - g_], b_S2[1 - g_]
                    Sn, bSn = S2[g_], b_S2[g_]
                    self.tt("dve", Sn, So, psKV[:, :].rearrange("p (h e) -> p h e", h=4), ALU.add, reads=bKV + [bSo], writes=[bSn])
                    self.tt("dve", Sn, Sn, ebl[:, :, c:c + 1].to_broadcast([128, 4, 128]), ALU.mult, reads=[bSn, b_ebl], writes=[bSn])
                    self.cp("act", Sbf2[g_], Sn, reads=[bSn], writes=[b_Sbf2[g_]])

                def back(ci):
                    c = order[ci]
                    cs = slice(c * 64, (c + 1) * 64)
                    d = ci % 2
                    gp = (self.sgen - 2) % 2 if True else 0
                    psOc, bOc = self.bank(6 + d)
                    for hh in range(4):
                        oc_ = psOc[:, hh * 64:(hh + 1) * 64]
                        self.mm(oc_, vc[0:64, c, hh * 128:(hh + 1) * 128], Abf[ci % 3][0:64, hh, :], True, False, reads=[b_vc, b_Abf[ci % 3]], writes=bOc)
                        self.mm(oc_, Sbf2[gp][:, hh, :], qt[:, hh, cs], False, True, reads=[b_Sbf2[gp], b_qt], writes=bOc)
                    o3 = psOc[:, 0:256].rearrange("p (h t) -> p h t", h=4)
                    if dirn == 0:
                        self.cp("act", ost[:, :, cs], o3, reads=bOc, writes=[b_ost])
                    else:
                        self.tt("dve", ost[:, :, cs], o3, ost[:, :, cs], ALU.add, reads=bOc + [b_ost], writes=[b_ost])

                stage1(0)
                stage1(1)
                stage2(0)
                for ci in range(NCH):
                    if ci + 2 < NCH:
                        stage1(ci + 2)
                    if ci + 1 < NCH:
                        stage2(ci + 1)
                    update(ci)
                    self.step_G()
                    back(ci)
                    self.step_tail()
                self.flush_G()
                if dirn == 0:
                    self.st(self.o1_d[:, :, c0:c0 + N].rearrange("k p n -> p k n"), ost[:, :, :N], reads=[b_ost], writes=[self.b_o1])
                    return
                self.flush_tail()
                self.pending = tail_gen(bi, c0, N, s, NT, seg0, segL, ost, b_ost)

            self.pending = None
            self.pendingG = G(0)
            self.flush_G()
            for bi in range(len(blocks)):
                self.pendingG = G(bi + 1) if bi + 1 < len(blocks) else None
                C(bi)
            self.flush_tail()

    def phase_FFN(self, l, blocks, final):
        P = self.P
        w1 = self.ab(KC * 2 * FH).rearrange("p (k n) -> p k n", k=KC); b_w1 = Buf()
        w2 = self.ab(HC * D).rearrange("p (k n) -> p k n", k=HC); b_w2 = Buf()
        g_region = self.af(HC * 256)
        stage = [g_region[:, 0:2816], g_region[:, 2816:5632]]; b_stage = [Buf(), Buf()]
        self.load_w(w1, self.ffn_w_in[l], KC, 2 * FH, stage, b_stage, b_w1, 2816)
        self.load_w(w2, self.ffn_w_out[l], HC, D, stage, b_stage, b_w2, D)
        self.P.barrier()
        gT = g_region.bitcast(BF16).rearrange("p (k n) -> p k n", k=HC); b_g = Buf()
        fT = self.af(KC * 512).rearrange("p (k n) -> p k n", k=KC); b_f = Buf()
        rstd = self.af(512); b_rstd = Buf()
        tmpk = [self.af(512), self.af(512)]; b_tmpk = [Buf(), Buf()]
        xk = [self.af(512), self.af(512)]; b_xk = [Buf(), Buf()]
        sqk = [self.ab(512), self.ab(512)]; b_sqk = [Buf(), Buf()]
        hT2 = [self.ab(KC * 512).rearrange("p (k n) -> p k n", k=KC) for _ in range(2)]; b_h2 = [Buf(), Buf()]
        ga = self.af(512); b_ga = Buf()
        psS, bpS = self.ps[7], self.bps[7]

        def finish_rstd(N):
            self.act(rstd[:, :N], psS[:, :N], ACT.Ln, reads=bpS + [self.b_c], writes=[b_rstd], scale=1.0 / D, bias=self.eps_t[:, 0:1])
            self.act(rstd[:, :N], rstd[:, :N], ACT.Exp, reads=[b_rstd], writes=[b_rstd], scale=-0.5)

        def gen_N(bi):
            c0, N = blocks[bi]
            s = 1 if c0 >= SEQ else 0
            hT, b_h = hT2[bi % 2], b_h2[bi % 2]
            A = self.dvec(l, s, 3); B = self.dvec(l, s, 4)
            xb_ = self.xbufs(c0, N)
            for kc in range(KC + 1):
                if kc < KC:
                    self.ld(xk[kc % 2][:, :N], self.xT_d[kc, :, c0:c0 + N], reads=xb_, writes=[b_xk[kc % 2]])
                    self.act(sqk[kc % 2][:, :N], xk[kc % 2][:, :N], ACT.Square, reads=[b_xk[kc % 2]], writes=[b_sqk[kc % 2]])
                if kc >= 1:
                    j = kc - 1
                    self.mm(psS[:, :N], self.ones_b, sqk[j % 2][:, :N], j == 0, j == KC - 1, reads=[b_sqk[j % 2], self.b_c], writes=bpS)
                yield
            finish_rstd(N)
            yield
            for kc in range(KC + 1):
                if kc < KC:
                    self.ld(xk[kc % 2][:, :N], self.xT_d[kc, :, c0:c0 + N], reads=xb_, writes=[b_xk[kc % 2]])
                if kc >= 1:
                    j = kc - 1
                    t_, bt_ = tmpk[j % 2], b_tmpk[j % 2]
                    self.tt("dve", t_[:, :N], xk[j % 2][:, :N], rstd[:, :N], ALU.mult, reads=[b_xk[j % 2], b_rstd], writes=[bt_])
                    if j % 2 == 0:
                        self.act(hT[:, j, :N], t_[:, :N], ACT.Identity, reads=[bt_, self.b_v1], writes=[b_h], scale=A[:, j:j + 1], bias=B[:, j:j + 1])
                    else:
                        self.ts("dve", hT[:, j, :N], t_[:, :N], A[:, j:j + 1], B[:, j:j + 1], ALU.mult, ALU.add, reads=[bt_, self.b_v1], writes=[b_h])
                yield

        def gen_R(bi):
            c0, N = blocks[bi]
            s = 1 if c0 >= SEQ else 0
            G = self.dvec(l, s, 5)
            xb_ = self.xbufs(c0, N)
            for kc in range(KC + 1):
                if kc < KC:
                    self.act(sqk[kc % 2][:, :N], fT[:, kc, :N], ACT.Square, reads=[b_f], writes=[b_sqk[kc % 2]])
                if kc >= 1:
                    j = kc - 1
                    self.mm(psS[:, :N], self.ones_b, sqk[j % 2][:, :N], j == 0, j == KC - 1, reads=[b_sqk[j % 2], self.b_c], writes=bpS)
                if kc % 2 == 0:
                    yield
            finish_rstd(N)
            yield
            for kc in range(KC + 1):
                if kc < KC:
                    self.ld(xk[kc % 2][:, :N], self.xT_d[kc, :, c0:c0 + N], reads=xb_, writes=[b_xk[kc % 2]])
                if kc >= 1:
                    j = kc - 1
                    t_, bt_ = tmpk[j % 2], b_tmpk[j % 2]
                    self.tt("dve", t_[:, :N], fT[:, j, :N], rstd[:, :N], ALU.mult, reads=[b_f, b_rstd], writes=[bt_])
                    self.stt("dve", fT[:, j, :N], t_[:, :N], G[:, j:j + 1], xk[j % 2][:, :N], ALU.mult, ALU.add,
                             reads=[bt_, b_xk[j % 2], self.b_v1, b_f], writes=[b_f])
                yield
            if not final:
                self.st(self.xT_src(c0, N), fT[:, :, :N], reads=[b_f], writes=xb_)
            else:
                for ti in range(N // 128):
                    for half in range(2):
                        ps2, bp2 = self.nextbank(4, 7)
                        for k in range(4):
                            kc = half * 4 + k
                            P.op("pe", lambda h, o_=ps2[:, k * 128:(k + 1) * 128], i_=fT[:, kc, ti * 128:(ti + 1) * 128]: h.transpose(o_, i_, self.ident_f),
                                 reads=[b_f, self.b_c], writes=bp2)
                        o2, bo2 = tmpk[half], b_tmpk[half]
                        self.cp("act" if half == 0 else "dve", o2, ps2[:, :], reads=bp2, writes=[bo2])
                        r0 = c0 + ti * 128
                        self.st(self.out[r0:r0 + 128, half * 512:(half + 1) * 512], o2, reads=[bo2], writes=[])
                    yield

        def step(name):
            g_ = self.ffn_gen.get(name)
            if g_ is None:
                return False
            try:
                next(g_)
            except StopIteration:
                self.ffn_gen[name] = None
            return True

        def flush(name):
            while self.ffn_gen.get(name) is not None:
                step(name)

        self.ffn_gen = {"N": gen_N(0), "R": None}
        flush("N")
        for bi, (c0, N) in enumerate(blocks):
            hT, b_h = hT2[bi % 2], b_h2[bi % 2]
            self.ffn_gen["N"] = gen_N(bi + 1) if bi + 1 < len(blocks) else None
            for hc in range(HC):
                psa, bpa = self.nextbank(0, 4)
                psb, bpb = self.nextbank(0, 4)
                for kc in range(KC):
                    self.mm(psa[:, :N], w1[:, kc, hc * 128:(hc + 1) * 128], hT[:, kc, :N], kc == 0, kc == KC - 1, reads=[b_w1, b_h], writes=bpa)
                for kc in range(KC):
                    self.mm(psb[:, :N], w1[:, kc, FH + hc * 128:FH + (hc + 1) * 128], hT[:, kc, :N], kc == 0, kc == KC - 1, reads=[b_w1, b_h], writes=bpb)
                self.act(ga[:, :N], psa[:, :N], ACT.Silu, reads=bpa, writes=[b_ga])
                self.tt("dve", gT[:, hc, :N], ga[:, :N], psb[:, :N], ALU.mult, reads=[b_ga] + bpb, writes=[b_g])
                if self.ffn_gen["R"] is not None:
                    step("R")
                else:
                    step("N")
            flush("R")
            for oc in range(KC):
                ps, bp = self.nextbank(4, 7)
                for hc in range(HC):
                    self.mm(ps[:, :N], w2[:, hc, oc * 128:(oc + 1) * 128], gT[:, hc, :N], hc == 0, hc == HC - 1, reads=[b_w2, b_g], writes=bp)
                self.cp("act" if oc % 2 == 0 else "dve", fT[:, oc, :N], ps[:, :N], reads=bp, writes=[b_f])
                step("N")
                step("N")
                step("N")
            flush("N")
            self.ffn_gen["R"] = gen_R(bi)
        flush("R")

    def phase_DE(self):
        P = self.P
        NKT = LT // 128
        KT = self.ab(2 * LT).rearrange("p (k n) -> p k n", k=2); b_KT = Buf()
        V = self.ab(NKT * 256).rearrange("p (t e) -> p t e", t=NKT); b_V = Buf()
        keepDE = self.ptr
        w = self.ab(KC * 1536).rearrange("p (k n) -> p k n", k=KC); b_w = Buf()
        stage = [self.af(1536), self.af(1536)]; b_stage = [Buf(), Buf()]
        self.load_w(w, self.att_w_in, KC, 1536, stage, b_stage, b_w, 1536)
        rstd = self.af(512); b_rstd = Buf()
        tmpk = [self.af(512), self.af(512)]; b_tmpk = [Buf(), Buf()]
        xk = [self.af(512), self.af(512)]; b_xk = [Buf(), Buf()]
        sqk = [self.ab(512), self.ab(512)]; b_sqk = [Buf(), Buf()]
        hT2 = [self.ab(KC * 512).rearrange("p (k n) -> p k n", k=KC) for _ in range(2)]; b_h2 = [Buf(), Buf()]
        psS, bpS = self.ps[7], self.bps[7]

        def gen_N(bi):
            c0, N, s = blocks[bi]
            hT_, b_h_ = hT2[bi % 2], b_h2[bi % 2]
            A = self.dvec(1, s, 0); B = self.dvec(1, s, 1)
            xb_ = self.xbufs(c0, N)
            for kc in range(KC + 1):
                if kc < KC:
                    self.ld(xk[kc % 2][:, :N], self.xT_d[kc, :, c0:c0 + N], reads=xb_, writes=[b_xk[kc % 2]])
                    self.act(sqk[kc % 2][:, :N], xk[kc % 2][:, :N], ACT.Square, reads=[b_xk[kc % 2]], writes=[b_sqk[kc % 2]])
                if kc >= 1:
                    j = kc - 1
                    self.mm(psS[:, :N], self.ones_b, sqk[j % 2][:, :N], j == 0, j == KC - 1, reads=[b_sqk[j % 2], self.b_c], writes=bpS)
                yield
            self.act(rstd[:, :N], psS[:, :N], ACT.Ln, reads=bpS + [self.b_c], writes=[b_rstd], scale=1.0 / D, bias=self.eps_t[:, 0:1])
            self.act(rstd[:, :N], rstd[:, :N], ACT.Exp, reads=[b_rstd], writes=[b_rstd], scale=-0.5)
            yield
            for kc in range(KC + 1):
                if kc < KC:
                    self.ld(xk[kc % 2][:, :N], self.xT_d[kc, :, c0:c0 + N], reads=xb_, writes=[b_xk[kc % 2]])
                if kc >= 1:
                    j = kc - 1
                    t_, bt_ = tmpk[j % 2], b_tmpk[j % 2]
                    self.tt("dve", t_[:, :N], xk[j % 2][:, :N], rstd[:, :N], ALU.mult, reads=[b_xk[j % 2], b_rstd], writes=[bt_])
                    if j % 2 == 0:
                        self.act(hT_[:, j, :N], t_[:, :N], ACT.Identity, reads=[bt_, self.b_v1], writes=[b_h_], scale=A[:, j:j + 1], bias=B[:, j:j + 1])
                    else:
                        self.ts("dve", hT_[:, j, :N], t_[:, :N], A[:, j:j + 1], B[:, j:j + 1], ALU.mult, ALU.add, reads=[bt_, self.b_v1], writes=[b_h_])
                yield

        def stepN(k=1):
            for _ in range(k):
                if self.pendN is not None:
                    try:
                        next(self.pendN)
                    except StopIteration:
                        self.pendN = None

        def flushN():
            while self.pendN is not None:
                stepN()

        cosb = self.af(512); sinb = self.af(512); b_cs = Buf()
        mk3 = lambda: [self.af(512) for _ in range(3)]
        raw = mk3(); b_raw = [Buf() for _ in range(3)]
        sq1 = [self.ab(512) for _ in range(3)]; b_sq1 = [Buf() for _ in range(3)]
        rs1 = mk3(); b_rs1 = [Buf() for _ in range(3)]
        kn = mk3(); b_kn = [Buf() for _ in range(3)]
        t1 = mk3(); b_t1 = [Buf() for _ in range(3)]
        t2 = mk3(); b_t2 = [Buf() for _ in range(3)]
        qst = self.ab(8 * 512).rearrange("p (k n) -> p k n", k=8); b_qst = Buf()
        blocks = [(SEQ, CTX, 1)] + [(i * 512, 512, 0) for i in range(SEQ // 512)]
        self.jbase = 0
        self.pendN = gen_N(0)
        flushN()
        for bi, (c0, N, s) in enumerate(blocks):
            NT = N // 128
            hT, b_h = hT2[bi % 2], b_h2[bi % 2]
            self.pendN = gen_N(bi + 1) if bi + 1 < len(blocks) else None
            if not s:
                self.ld(cosb, self.cos_in[:, c0:c0 + N], writes=[b_cs])
                self.ld(sinb, self.sin_in[:, c0:c0 + N], writes=[b_cs])
            heads = [("k", kv) for kv in range(2)]
            if (not s) and c0 < OWN:
                heads += [("q", hq) for hq in range(8)]

            def hinfo(j):
                kind, hx = heads[j]
                d = (self.jbase + j) % 3
                col = (1024 + hx * 128) if kind == "k" else hx * 128
                gcol = 86 if kind == "k" else 85
                dst = KT[:, hx, c0:c0 + N] if kind == "k" else qst[:, hx, :N]
                bdst = b_KT if kind == "k" else b_qst
                return d, col, gcol, dst, bdst

            def s1(j):
                d, col, gcol, dst, bdst = hinfo(j)
                ps, bp = self.nextbank(0, 4)
                for kc in range(KC):
                    self.mm(ps[:, :N], w[:, kc, col:col + 128], hT[:, kc, :N], kc == 0, kc == KC - 1, reads=[b_w, b_h], writes=bp)
                self.cp("dve", raw[d][:, :N], ps[:, :N], reads=bp, writes=[b_raw[d]])
                self.act(sq1[d][:, :N], raw[d][:, :N], ACT.Square, reads=[b_raw[d]], writes=[b_sq1[d]])

            def s2(j):
                d, col, gcol, dst, bdst = hinfo(j)
                ps2, bp2 = self.nextbank(4, 6)
                self.mm(ps2[:, :N], self.ones_b, sq1[d][:, :N], True, True, reads=[b_sq1[d], self.b_c], writes=bp2)
                self.act(rs1[d][:, :N], ps2[:, :N], ACT.Ln, reads=bp2 + [self.b_c], writes=[b_rs1[d]], scale=1.0 / 128, bias=self.eps_t[:, 0:1])
                self.act(rs1[d][:, :N], rs1[d][:, :N], ACT.Exp, reads=[b_rs1[d]], writes=[b_rs1[d]], scale=-0.5)
                if s:
                    self.stt("dve", dst, raw[d][:, :N], self.vT2[:, gcol:gcol + 1], rs1[d][:, :N], ALU.mult, ALU.mult,
                             reads=[b_raw[d], b_rs1[d], self.b_v1], writes=[bdst])
                else:
                    self.stt("dve", kn[d][:, :N], raw[d][:, :N], self.vT2[:, gcol:gcol + 1], rs1[d][:, :N], ALU.mult, ALU.mult,
                             reads=[b_raw[d], b_rs1[d], self.b_v1], writes=[b_kn[d]])

            def s3(j):
                d, col, gcol, dst, bdst = hinfo(j)
                if s:
                    return
                ps3, bp3 = self.nextbank(6, 7)
                self.mm(ps3[:, :N], self.rotT, kn[d][:, :N], True, True, reads=[b_kn[d], self.b_c], writes=bp3)
                self.tt("pool", t1[d][:, :N], kn[d][:, :N], cosb[:, :N], ALU.mult, reads=[b_kn[d], b_cs], writes=[b_t1[d]])
                self.tt("dve", t2[d][:, :N], ps3[:, :N], sinb[:, :N], ALU.mult, reads=bp3 + [b_cs], writes=[b_t2[d]])
                self.tt("pool", dst, t1[d][:, :N], t2[d][:, :N], ALU.add, reads=[b_t1[d], b_t2[d]], writes=[bdst])

            nj = len(heads)
            for i in range(nj + 2):
                if i < nj:
                    s1(i)
                if 0 <= i - 1 < nj:
                    s2(i - 1)
                if 0 <= i - 2 < nj:
                    s3(i - 2)
                stepN(2)
            self.jbase += nj
            if (not s) and c0 < OWN:
                self.st(self.q1_d[:, :, c0:c0 + N].rearrange("k p n -> p k n"), qst[:, :, :N], reads=[b_qst], writes=[self.b_q1])
            for ti in range(NT):
                ps, bp = self.nextbank(0, 4)
                for kc in range(KC):
                    self.mm(ps[:, 0:256], hT[:, kc, ti * 128:(ti + 1) * 128], w[:, kc, 1280:1536], kc == 0, kc == KC - 1, reads=[b_w, b_h], writes=bp)
                self.cp("act", V[:, (c0 // 128) + ti, :], ps[:, 0:256], reads=bp, writes=[b_V])
                stepN(2)
            flushN()
        self.P.barrier()
        self.ptr = keepDE
        wo = self.ab(KC * D).rearrange("p (k n) -> p k n", k=KC); b_wo = Buf()
        p_st = self.ptr
        stage = [self.af(1024), self.af(1024)]; b_stage = [Buf(), Buf()]
        self.load_w(wo, self.att_w_out, KC, D, stage, b_stage, b_wo, D)
        self.P.barrier()
        self.ptr = p_st
        qb = self.ab(8 * 512).rearrange("p (k n) -> p k n", k=8); b_qb = Buf()
        xT = self.af(KC * 512).rearrange("p (k n) -> p k n", k=KC); b_x = Buf()
        pt = [self.ab(1024) for _ in range(3)]; b_pt = [Buf() for _ in range(3)]
        acc = self.af(1024); b_acc = Buf()
        accp = self.af(1024); b_accp = Buf()
        ones_f = self.af(128); b_of = Buf()
        P.op("pool", lambda h: h.memset(ones_f, 1.0), writes=[b_of])
        rs = self.af(512); b_rs = Buf()
        attnT = self.ab(8 * 512).rearrange("p (k n) -> p k n", k=8); b_at = Buf()
        yT = self.af(KC * 512).rearrange("p (k n) -> p k n", k=KC); b_y = Buf()
        sq = self.ab(KC * 512).rearrange("p (k n) -> p k n", k=KC); b_sq = Buf()
        rstd = self.af(512); b_rstd = Buf()
        xo = self.af(KC * 512).rearrange("p (k n) -> p k n", k=KC); b_xo = Buf()
        scale = 128 ** -0.5
        N = 512
        NKP = NKT // 2
        for qi in range(OWN // 512):
            c0 = qi * 512
            self.ld(qb, self.q1_d[:, :, c0:c0 + N].rearrange("k p n -> p k n"), reads=[self.b_q1], writes=[b_qb])
            self.ld(xT, self.xT_src(c0, N), reads=self.xbufs(c0, N), writes=[b_x])
            seq = [(hq, kp) for hq in range(8) for kp in range(NKP)]

            def issue_S(idx):
                hq, kp = seq[idx]
                kv = hq // 4
                b0 = (idx % 2) * 2
                for j in range(2):
                    kt_ = 2 * kp + j
                    self.mm(self.ps[b0 + j][:, :], KT[:, kv, kt_ * 128:(kt_ + 1) * 128], qb[:, hq, :], True, True,
                            reads=[b_KT, b_qb], writes=self.bps[b0 + j])
                self.act(pt[idx % 3], self.psall[:, b0 * 512:(b0 + 2) * 512], ACT.Exp, reads=self.bps[b0] + self.bps[b0 + 1],
                         writes=[b_pt[idx % 3]], scale=scale)

            issue_S(0)
            for idx, (hq, kp) in enumerate(seq):
                kv = hq // 4
                psO, bO = self.bank(4 + hq % 2)
                psL, bL = self.bank(6 + hq % 2)
                if idx + 1 < len(seq):
                    issue_S(idx + 1)
                p_, bp_ = pt[idx % 3], b_pt[idx % 3]
                for j in range(2):
                    kt_ = 2 * kp + j
                    self.mm(psO[:, :], V[:, kt_, kv * 128:(kv + 1) * 128], p_[:, j * 512:(j + 1) * 512], kt_ == 0, kt_ == NKT - 1,
                            reads=[b_V, bp_], writes=bO)
                if kp == 0:
                    self.cp("dve", acc, p_, reads=[bp_], writes=[b_acc])
                elif kp % 4 == 3:
                    self.mm(psL[:, :], self.ones_b, p_[:, 0:512], kp == 3, False, reads=[self.b_c, bp_], writes=bL)
                    self.mm(psL[:, :], self.ones_b, p_[:, 512:1024], False, False, reads=[self.b_c, bp_], writes=bL)
                else:
                    self.tt("dve", acc, acc, p_, ALU.add, reads=[bp_, b_acc], writes=[b_acc])
                if kp == NKP - 1:
                    self.mm(psL[:, :], ones_f, acc[:, 0:512], False, False, reads=[b_of, b_acc], writes=bL)
                    self.mm(psL[:, :], ones_f, acc[:, 512:1024], False, True, reads=[b_of, b_acc], writes=bL)
                    self.act(rs, psL[:, :], ACT.Ln, reads=bL, writes=[b_rs])
                    self.act(rs, rs, ACT.Exp, reads=[b_rs], writes=[b_rs], scale=-1.0)
                    self.tt("dve", attnT[:, hq, :], psO[:, :], rs, ALU.mult, reads=bO + [b_rs], writes=[b_at])
            for oc in range(KC):
                ps, bp = self.nextbank(0, 4)
                for kc in range(KC):
                    self.mm(ps[:, :], wo[:, kc, oc * 128:(oc + 1) * 128], attnT[:, kc, :], kc == 0, kc == KC - 1, reads=[b_wo, b_at], writes=bp)
                self.cp("act" if oc % 2 == 0 else "dve", yT[:, oc, :], ps[:, :], reads=bp, writes=[b_y])
            self.residual(yT, b_y, xT, b_x, N, 1, 0, 1, sq, b_sq, rstd, b_rstd, yT, b_y, xo, b_xo)
            self.st(self.xT_src(c0, N), xo, reads=[b_xo], writes=self.xbufs(c0, N))


def _pool_consts(mirror):
    L = 384
    wins = (2, 4, 8, 16)
    pm = np.zeros((3, 4, 3, 128, 128), np.float32)
    invc = np.zeros((3, 4, 128), np.float32)
    for tt in range(3):
        for g, w in enumerate(wins):
            for t in range(128):
                T = tt * 128 + t
                lo = T - w // 2 + (1 if mirror else 0)
                hi = lo + w
                lo = max(lo, 0); hi = min(hi, L)
                cnt = hi - lo
                invc[tt, g, t] = 1.0 / cnt
                for S in range(lo, hi):
                    r = S // 128 - tt
                    pm[tt, g, r + 1, S % 128, t] += 1.0
                pm[tt, g, 1, t, t] -= cnt
    pm_l = np.ascontiguousarray(pm.transpose(3, 0, 1, 2, 4).reshape(128, 36 * 128))
    invc_l = np.ascontiguousarray(np.broadcast_to(invc.reshape(1, 12 * 128), (128, 12 * 128)))
    return pm_l, invc_l


def _rope_tables(flip):
    j = np.arange(SEQ)
    t = (SEQ - 1 - j) if flip else j
    row = (t // 64).astype(np.float32)
    colp = (t % 64).astype(np.float32)
    inv = (np.float32(10000.0) ** (-(np.arange(32, dtype=np.float32) / np.float32(32)))).astype(np.float32)
    cosT = np.zeros((128, SEQ), np.float32); sinT = np.zeros((128, SEQ), np.float32)
    for p in range(128):
        pos = row if p < 64 else colp
        ang = (pos * inv[p % 32]).astype(np.float32)
        cosT[p] = np.cos(ang); sinT[p] = np.sin(ang)
    return cosT, sinT


def _consts():
    c = np.zeros((128, 896), np.float32)
    c[:, 0:128] = np.eye(128, dtype=np.float32)
    R = np.zeros((128, 128), np.float32)
    for p in range(128):
        if (p // 32) % 2 == 0:
            R[p, p + 32] = -1.0
        else:
            R[p, p - 32] = 1.0
    c[:, 128:256] = R.T
    s = np.arange(64)[:, None]; t = np.arange(64)[None, :]
    c[0:64, 256:320] = (s <= t).astype(np.float32)
    c[0:64, 320:384] = (s >= t).astype(np.float32)
    m = np.ones(512, np.float32); m[::64] = 0.0
    c[:, 384:896] = m[None, :]
    return c


_CACHE = {}


def _get_nc(dbg=None, stop_after=None):
    key = (tuple(dbg) if dbg else None, stop_after)
    if key not in _CACHE:
        _CACHE[key] = Builder(dbg=dbg, stop_after=stop_after).build()
    return _CACHE[key]


def make_in_maps(x, c, ctx, c_ctx, ada_w, ada_b, norm_g, ab_w_in, ab_w_out, pool_w, pool_scale,
                 hg_lower, hg_onorm_g, att_w_in, att_w_out, att_qnorm_g, att_knorm_g, ffn_w_in, ffn_w_out):
    f = lambda a: np.ascontiguousarray(np.asarray(a, dtype=np.float32))
    x = f(x); c = f(c); ctx = f(ctx); c_ctx = f(c_ctx); ada_w = f(ada_w); ada_b = f(ada_b); norm_g = f(norm_g)
    ab_w_in = f(ab_w_in); ab_w_out = f(ab_w_out); pool_w = f(pool_w); pool_scale = f(pool_scale); hg_lower = f(hg_lower)
    consts = _consts()
    shared = {
        "ada_w": ada_w, "w_out0": f(ab_w_out[0]), "pool_w": f(pool_w[0]), "att_w_in": f(att_w_in[0]),
        "att_w_out": f(att_w_out[0]), "ffn_w_in": f(ffn_w_in), "ffn_w_out": f(ffn_w_out), "consts": consts,
    }
    per_h = []
    for h in range(2):
        w_in0 = ab_w_in[0].copy()
        hl = hg_lower.copy()
        if h == 1:
            w_in0[:, 1024:1536] = ab_w_in[0][:, 1536:2048]
            w_in0[:, 1536:2048] = ab_w_in[0][:, 1024:1536]
            hl = hl[:, ::-1, :]
        pm_l, invc_l = _pool_consts(mirror=(h == 1))
        cosT, sinT = _rope_tables(flip=(h == 1))
        v2 = np.zeros((87, 128), np.float32)
        v2[0:64] = norm_g.reshape(64, 128)
        v2[64:68] = pool_scale[0].reshape(4, 128)
        v2[68:76] = hl[0].reshape(8, 128)
        v2[76:84] = hl[1].reshape(8, 128)
        v2[84] = f(hg_onorm_g)[0]
        v2[85] = f(att_qnorm_g)[0]
        v2[86] = f(att_knorm_g)[0]
        per_h.append({"w_in0": np.ascontiguousarray(w_in0), "pmats": pm_l, "invcnt": invc_l, "cosT": cosT, "sinT": sinT, "vecs2": v2})
    in_maps = []
    for core in range(NCORES):
        b, h = divmod(core, 2)
        xl = x[b] if h == 0 else x[b, ::-1]
        cl = ctx[b] if h == 0 else ctx[b, ::-1]
        v1 = np.zeros((112, 128), np.float32)
        v1[0:8] = c[b].reshape(8, 128)
        v1[8:16] = c_ctx.reshape(8, 128)
        v1[16:112] = ada_b.reshape(96, 128)
        m = dict(shared)
        m.update(per_h[h])
        m["x_loc"] = np.ascontiguousarray(xl)
        m["ctx_loc"] = np.ascontiguousarray(cl)
        m["vecs1"] = v1
        in_maps.append(m)
    return in_maps


def kernel(**inputs):
    nc = _get_nc()
    in_maps = make_in_maps(**inputs)
    res = run_bass_kernel_spmd(nc, in_maps, core_ids=list(range(NCORES)))
    out = np.zeros((4, SEQ, D), np.float32)
    for core in range(NCORES):
        b, h = divmod(core, 2)
        o = np.asarray(res.results[core]["out_loc"], dtype=np.float32)
        if h == 0:
            out[b, 0:OWN] = o
        else:
            out[b, OWN:SEQ] = o[::-1]
    return out
```

```python
import contextlib
import numpy as np
import ml_dtypes
import concourse.bass as bass
import concourse.mybir as mybir
from concourse.bass_utils import run_bass_kernel_spmd

ACT = mybir.ActivationFunctionType
ALU = mybir.AluOpType
F32 = mybir.dt.float32
BF16 = mybir.dt.bfloat16

D = 1024
KC = 8
SEQ = 8192
CTX = 256
LT = SEQ + CTX
OWN = 4096
FH = 2816
HC = 22
EPS = 1e-6
NCORES = 8


class Tok:
    __slots__ = ("sem", "val")

    def __init__(self, sem, val):
        self.sem = sem
        self.val = val


class Buf:
    def __init__(self, name=""):
        self.name = name
        self.w = None
        self.r = {}


class Eng:
    def __init__(self, name):
        self.name = name
        self.ops = []
        self.count = 0
        self.sem = None
        self.seen = {}


class Prog:
    ENGS = ("pe", "act", "dve", "pool", "sp")

    def __init__(self, nc, n_slots=None):
        self.nc = nc
        self.stack = contextlib.ExitStack()
        self.engs = {n: Eng(n) for n in self.ENGS}
        for n, e in self.engs.items():
            e.sem = self.stack.enter_context(nc.semaphore("s_" + n))
        n_slots = n_slots or {"sp": 8, "pool": 6}
        self.slots = {}
        self.slot_rr = {}
        for q, k in n_slots.items():
            self.slots[q] = [[self.stack.enter_context(nc.semaphore(f"d_{q}{i}")), 0] for i in range(k)]
            self.slot_rr[q] = 0

    def sbuf(self, name, shape, dtype):
        return self.stack.enter_context(self.nc.sbuf_tensor(name, list(shape), dtype))

    def psum(self, name, shape, dtype):
        return self.stack.enter_context(self.nc.psum_tensor(name, list(shape), dtype))

    def _collect(self, e, reads, writes):
        waits = {}

        def need(tok):
            if tok is None:
                return
            if tok.sem is e.sem and e.name == "pe":
                return
            k = id(tok.sem)
            if k not in waits or waits[k].val < tok.val:
                waits[k] = tok

        for b in reads:
            need(b.w)
        for b in writes:
            need(b.w)
            for t in b.r.values():
                need(t)
        return waits

    def _emit_waits(self, e, waits):
        for k, tok in waits.items():
            if e.seen.get(k, 0) >= tok.val:
                continue
            e.seen[k] = tok.val
            e.ops.append(("wait", tok.sem, tok.val))

    def _mark(self, tok, reads, writes):
        k = id(tok.sem)
        for b in reads:
            b.r[k] = tok
        for b in writes:
            b.w = tok
            b.r = {}

    def op(self, eng, fn, reads=(), writes=()):
        e = self.engs[eng]
        self._emit_waits(e, self._collect(e, reads, writes))
        e.count += 1
        tok = Tok(e.sem, e.count)
        e.ops.append(("op", fn, e.sem, 1))
        self._mark(tok, reads, writes)
        return tok

    def dma(self, q, out, in_, reads=(), writes=()):
        e = self.engs[q]
        i = self.slot_rr[q]
        self.slot_rr[q] = (i + 1) % len(self.slots[q])
        slot = self.slots[q][i]
        waits = self._collect(e, reads, writes)
        if slot[1] > 0:
            k = id(slot[0])
            prev = Tok(slot[0], 16 * slot[1])
            if k not in waits or waits[k].val < prev.val:
                waits[k] = prev
        self._emit_waits(e, waits)
        slot[1] += 1
        tok = Tok(slot[0], 16 * slot[1])
        e.ops.append(("op", lambda h, o=out, i_=in_: h.dma_start(out=o, in_=i_), slot[0], 16))
        self._mark(tok, reads, writes)
        return tok

    def barrier(self):
        toks = []
        for e in self.engs.values():
            if e.count:
                toks.append(Tok(e.sem, e.count))
        for q, sl in self.slots.items():
            for s in sl:
                if s[1]:
                    toks.append(Tok(s[0], 16 * s[1]))
        for e in self.engs.values():
            w = {}
            for t in toks:
                if t.sem is e.sem:
                    continue
                w[id(t.sem)] = t
            self._emit_waits(e, w)

    def finish(self):
        self.barrier()
        with self.nc.Block() as block:
            def mk(name):
                def body(h):
                    for o in self.engs[name].ops:
                        if o[0] == "wait":
                            h.wait_ge(o[1], o[2])
                        else:
                            o[1](h).then_inc(o[2], o[3])
                return body
            block.tensor(mk("pe"))
            block.scalar(mk("act"))
            block.vector(mk("dve"))
            block.gpsimd(mk("pool"))
            block.sync(mk("sp"))
        self.stack.close()


ARENA = 53184


class Builder:
    def __init__(self, dbg=None, stop_after=None):
        self.dbg = dbg
        self.stop_after = stop_after
        nc = bass.Bass("TRN2", target_bir_lowering=False)
        self.nc = nc
        di = lambda n, s, dt=F32: nc.dram_tensor(n, list(s), dt, kind="ExternalInput").ap()
        self.x_in = di("x_loc", [SEQ, D])
        self.ctx_in = di("ctx_loc", [CTX, D])
        self.vecs1 = di("vecs1", [112, 128])
        self.vecs2 = di("vecs2", [87, 128])
        self.ada_w = di("ada_w", [2, D, 6 * D])
        self.w_in0 = di("w_in0", [D, 3072])
        self.w_out0 = di("w_out0", [D, D])
        self.pool_w = di("pool_w", [4, 128, 128])
        self.att_w_in = di("att_w_in", [D, 1536])
        self.att_w_out = di("att_w_out", [D, D])
        self.ffn_w_in = di("ffn_w_in", [2, D, 2 * FH])
        self.ffn_w_out = di("ffn_w_out", [2, FH, D])
        self.pm_in = di("pmats", [128, 36 * 128])
        self.invc_in = di("invcnt", [128, 12 * 128])
        self.cos_in = di("cosT", [128, SEQ])
        self.sin_in = di("sinT", [128, SEQ])
        self.cst_in = di("consts", [128, 128 * 2 + 64 * 2 + 512])
        self.out = nc.dram_tensor("out_loc", [OWN, D], F32, kind="ExternalOutput").ap()
        ds = lambda n, s, dt: nc.dram_tensor(n, list(s), dt).ap()
        self.xT_d = ds("xT_d", [KC, 128, LT], F32)
        self.qs_d = ds("qs_d", [4, 128, LT], BF16)
        self.z_d = ds("z_d", [8, 128, LT], F32)
        self.gs_d = ds("gs_d", [4, 128, LT], BF16)
        self.u_d = ds("u_d", [LT, 512], BF16)
        self.v_d = ds("v_d", [LT, 512], BF16)
        self.o1_d = ds("o1_d", [4, 128, LT], F32)
        self.q1_d = ds("q1_d", [8, 128, OWN], BF16)
        self.b_xTg = [Buf(f"xT{i}") for i in range(LT // 256)]; self.b_xT = None; self.b_qs = Buf(); self.b_z = Buf(); self.b_gs = Buf()
        self.b_u = Buf(); self.b_v = Buf(); self.b_o1 = Buf(); self.b_q1 = Buf()
        self.dbg_out = {}
        if dbg:
            for name in dbg:
                if name == "dv":
                    continue
                self.dbg_out[name] = nc.dram_tensor("dbg_" + name, [KC, 128, LT], F32, kind="ExternalOutput").ap()
        self.P = Prog(nc)
        self.arena = self.P.sbuf("arena", [128, ARENA], F32)
        self.ptr = 0
        self.psall = self.P.psum("psall", [128, 4096], F32)
        self.ps = [self.psall[:, i * 512:(i + 1) * 512] for i in range(8)]
        self.bps = [[Buf(f"ps{i}")] for i in range(8)]
        self.rr = 0

    def af(self, n):
        a = self.arena[:, self.ptr:self.ptr + n]
        self.ptr += n
        assert self.ptr <= ARENA, f"arena overflow {self.ptr}"
        return a

    def ab(self, n):
        assert n % 2 == 0
        return self.af(n // 2).bitcast(BF16)

    def phase_end(self, keep):
        self.P.barrier()
        self.ptr = keep

    def mm(self, out, lhsT, rhs, start, stop, reads, writes, **kw):
        self.P.op("pe", lambda h: h.matmul(out, lhsT=lhsT, rhs=rhs, start=start, stop=stop, **kw), reads=reads, writes=writes)

    def act(self, out, in_, func, reads, writes, scale=1.0, bias=0.0):
        self.P.op("act", lambda h: h.activation(out=out, in_=in_, func=func, scale=scale, bias=bias), reads=reads, writes=writes)

    def tt(self, eng, out, in0, in1, op, reads, writes):
        self.P.op(eng, lambda h: h.tensor_tensor(out=out, in0=in0, in1=in1, op=op), reads=reads, writes=writes)

    def ts(self, eng, out, in0, s1, s2, op0, op1, reads, writes):
        self.P.op(eng, lambda h: h.tensor_scalar(out=out, in0=in0, scalar1=s1, scalar2=s2, op0=op0, op1=op1), reads=reads, writes=writes)

    def stt(self, eng, out, in0, scalar, in1, op0, op1, reads, writes):
        self.P.op(eng, lambda h: h.scalar_tensor_tensor(out=out, in0=in0, scalar=scalar, in1=in1, op0=op0, op1=op1), reads=reads, writes=writes)

    def cp(self, eng, out, in_, reads, writes):
        if eng == "act":
            self.P.op("act", lambda h: h.copy(out=out, in_=in_), reads=reads, writes=writes)
        else:
            self.P.op(eng, lambda h: h.tensor_copy(out=out, in_=in_), reads=reads, writes=writes)

    def ld(self, out, in_, reads=(), writes=()):
        self.P.dma("sp", out, in_, reads=reads, writes=writes)

    def st(self, out, in_, reads=(), writes=()):
        self.P.dma("pool", out, in_, reads=reads, writes=writes)

    def bank(self, i):
        return self.ps[i], self.bps[i]

    def tailbank(self):
        i = (0, 1, 2, 3, 6, 7)[self.rr % 6]
        self.rr += 1
        return self.ps[i], self.bps[i]

    def nextbank(self, lo=0, hi=8):
        i = lo + self.rr % (hi - lo)
        self.rr += 1
        return self.ps[i], self.bps[i]

    def load_w(self, dst, src2d, kcs, ncols, stage, bstage, bdst, SC):
        i = 0
        for kc in range(kcs):
            for c in range(0, ncols, SC):
                w = min(SC, ncols - c)
                s, bs = stage[i % 2], bstage[i % 2]
                self.ld(s[:, :w], src2d[kc * 128:(kc + 1) * 128, c:c + w], writes=[bs])
                self.cp("act" if i % 2 == 0 else "dve", dst[:, kc, c:c + w], s[:, :w], reads=[bs], writes=[bdst])
                i += 1

    def rstd_of(self, src3, bsrc, kcs, N, dim, sq, bsq, rstd, brstd, scale_in=None):
        self.act(sq[:, :kcs, :N], src3, ACT.Square, reads=bsrc, writes=[bsq])
        ps, bp = self.tailbank() if getattr(self, "in_tail", False) else self.nextbank(4, 8)
        for kc in range(kcs):
            self.mm(ps[:, :N], self.ones_b, sq[:, kc, :N], kc == 0, kc == kcs - 1, reads=[bsq, self.b_c], writes=bp)
        self.act(rstd[:, :N], ps[:, :N], ACT.Ln, reads=bp + [self.b_c], writes=[brstd], scale=1.0 / dim, bias=self.eps_t[:, 0:1])
        self.act(rstd[:, :N], rstd[:, :N], ACT.Exp, reads=[brstd], writes=[brstd], scale=-0.5)

    def build(self):
        P = self.P
        cst = self.af(896); self.b_c = Buf("consts")
        self.ld(cst, self.cst_in, writes=[self.b_c])
        self.ident_f = cst[:, 0:128]
        self.rotT = cst[:, 128:256]
        self.mask_f = [cst[0:64, 256:320], cst[0:64, 320:384]]
        self.cmask = cst[:, 384:896]
        self.ident_b = self.ab(128)
        self.ones_b = self.ab(128)
        self.eps_t = self.af(2)
        self.cp("dve", self.ident_b, self.ident_f, reads=[self.b_c], writes=[self.b_c])
        P.op("pool", lambda h: h.memset(self.ones_b, 1.0), writes=[self.b_c])
        P.op("pool", lambda h: h.memset(self.eps_t, EPS), writes=[self.b_c])
        self.vT1 = self.af(112); self.vT2 = self.af(88)
        self.dv = self.af(2 * 2 * 6 * 8)
        self.hv = self.af(24)
        self.b_v1 = Buf("vecs")
        self.phase_M()
        keep0 = self.ptr
        if self.dbg and "dv" in self.dbg:
            dvo = self.nc.dram_tensor("dbg_dvec", [128, 216], F32, kind="ExternalOutput").ap()
            self.P.dma("sp", dvo[:, 0:192], self.dv, reads=[self.b_v1])
            self.P.dma("sp", dvo[:, 192:216], self.hv, reads=[self.b_v1])
        if self.stop_after == "M":
            return self.finish()
        self.phase_A()
        self.phase_end(keep0)
        if self.stop_after == "A":
            return self.finish()
        self.phase_B()
        self.phase_end(keep0)
        if self.dbg and "mid0" in self.dbg:
            self.P.dma("sp", self.dbg_out["mid0"], self.xT_d, reads=self.b_xTg)
        if self.stop_after == "B":
            return self.finish()
        self.phase_FFN(0, [(SEQ, CTX)] + [(i * 512, 512) for i in range(SEQ // 512)], final=False)
        self.phase_end(keep0)
        if self.dbg and "out0" in self.dbg:
            self.P.dma("sp", self.dbg_out["out0"], self.xT_d, reads=self.b_xTg)
        if self.stop_after == "C":
            return self.finish()
        self.phase_DE()
        self.phase_end(keep0)
        if self.dbg and "mid1" in self.dbg:
            self.P.dma("sp", self.dbg_out["mid1"], self.xT_d, reads=self.b_xTg)
        if self.stop_after == "E":
            return self.finish()
        self.phase_FFN(1, [(i * 512, 512) for i in range(OWN // 512)], final=True)
        return self.finish()

    def finish(self):
        self.P.finish()
        return self.nc

    def dvec(self, l, s, part):
        o = ((l * 2 + s) * 6 + part) * 8
        return self.dv[:, o:o + 8]

    def phase_M(self):
        P = self.P
        p0 = self.ptr
        r1 = self.af(128); r2 = self.af(128); b_r = Buf()
        self.ld(r1[0:112, :], self.vecs1, writes=[b_r])
        self.ld(r2[0:87, :], self.vecs2, writes=[b_r])
        ps, bp = self.bank(0)
        P.op("pe", lambda h: h.transpose(ps[:, 0:112], r1[0:112, :], self.ident_f[0:112, 0:112]), reads=[b_r, self.b_c], writes=bp)
        P.op("pe", lambda h: h.transpose(ps[:, 128:215], r2[0:87, :], self.ident_f[0:87, 0:87]), reads=[b_r, self.b_c], writes=bp)
        self.cp("dve", self.vT1, ps[:, 0:112], reads=bp, writes=[self.b_v1])
        self.cp("dve", self.vT2[:, 0:87], ps[:, 128:215], reads=bp, writes=[self.b_v1])
        scr = self.af(16); b_scr = Buf()
        scr3 = scr.rearrange("p (k s) -> p k s", s=2)
        self.act(scr3[:, :, 0], self.vT1[:, 0:8], ACT.Silu, reads=[self.b_v1], writes=[b_scr])
        self.act(scr3[:, :, 1], self.vT1[:, 8:16], ACT.Silu, reads=[self.b_v1], writes=[b_scr])
        lb = self.hv[:, 0:8]; oml = self.hv[:, 8:16]; noml = self.hv[:, 16:24]
        b_hv = self.b_v1
        self.tt("dve", lb, self.vT2[:, 68:76], self.vT2[:, 76:84], ALU.subtract, reads=[self.b_v1], writes=[b_hv])
        self.act(lb, lb, ACT.Sigmoid, reads=[b_hv], writes=[b_hv])
        self.ts("dve", oml, lb, -1.0, 1.0, ALU.mult, ALU.add, reads=[b_hv], writes=[b_hv])
        self.ts("dve", noml, oml, -1.0, 0.0, ALU.mult, ALU.add, reads=[b_hv], writes=[b_hv])
        aw = [self.af(6144), self.af(6144)]; b_aw = [Buf(), Buf()]
        mod = self.af(2 * 96); b_mod = Buf()
        i = 0
        macc = self.af(96); b_macc = Buf()
        for l in range(2):
            banks = [self.bank(1), self.bank(2)]
            for kc in range(KC):
                a, ba = aw[i % 2], b_aw[i % 2]
                psm, bpm = banks[kc // 4]
                self.ld(a, self.ada_w[l, kc * 128:(kc + 1) * 128, :], writes=[ba])
                for j in range(48):
                    o_ = (kc % 4) * 96 + 2 * j
                    self.mm(psm[:, o_:o_ + 2], a[:, j * 128:(j + 1) * 128], scr3[:, kc, :], True, True,
                            reads=[ba, b_scr], writes=bpm)
                i += 1
            self.cp("dve", macc, banks[0][0][:, 0:96], reads=banks[0][1], writes=[b_macc])
            self.tt("dve", macc, macc, banks[0][0][:, 96:192], ALU.add, reads=banks[0][1] + [b_macc], writes=[b_macc])
            self.tt("dve", macc, macc, banks[0][0][:, 192:288], ALU.add, reads=banks[0][1] + [b_macc], writes=[b_macc])
            self.tt("dve", macc, macc, banks[0][0][:, 288:384], ALU.add, reads=banks[0][1] + [b_macc], writes=[b_macc])
            for q_ in range(4):
                self.tt("dve", macc, macc, banks[1][0][:, q_ * 96:(q_ + 1) * 96], ALU.add, reads=banks[1][1] + [b_macc], writes=[b_macc])
            m3 = mod[:, l * 96:(l + 1) * 96].rearrange("p (j s) -> p j s", s=2)
            self.tt("dve", m3, macc.rearrange("p (j s) -> p j s", s=2),
                    self.vT1[:, 16 + l * 48:16 + (l + 1) * 48].unsqueeze(2).to_broadcast([128, 48, 2]), ALU.add,
                    reads=[b_macc, self.b_v1], writes=[b_mod])
            for s in range(2):
                part = lambda k: m3[:, k * 8:(k + 1) * 8, s]
                ng = lambda i_: self.vT2[:, (l * 4 + i_) * 8:(l * 4 + i_) * 8 + 8]
                rd = [b_mod, self.b_v1]
                self.stt("dve", self.dvec(l, s, 0), part(1), 1.0, ng(0), ALU.add, ALU.mult, reads=rd, writes=[self.b_v1])
                self.cp("dve", self.dvec(l, s, 1), part(0), reads=rd, writes=[self.b_v1])
                self.tt("dve", self.dvec(l, s, 2), part(2), ng(1), ALU.mult, reads=rd, writes=[self.b_v1])
                self.stt("dve", self.dvec(l, s, 3), part(4), 1.0, ng(2), ALU.add, ALU.mult, reads=rd, writes=[self.b_v1])
                self.cp("dve", self.dvec(l, s, 4), part(3), reads=rd, writes=[self.b_v1])
                self.tt("dve", self.dvec(l, s, 5), part(5), ng(3), ALU.mult, reads=rd, writes=[self.b_v1])
        self.P.barrier()
        self.ptr = p0

    def norm_mod(self, x3, bx, N, l, s, which, sq, bsq, rstd, brstd, tmp, btmp, hT, bhT):
        self.rstd_of(x3, [bx], KC, N, D, sq, bsq, rstd, brstd)
        self.tt("dve", tmp[:, :, :N], x3, rstd[:, :N].unsqueeze(1).to_broadcast([128, KC, N]), ALU.mult,
                reads=[bx, brstd], writes=[btmp])
        A = self.dvec(l, s, 0 if which == 1 else 3)
        B = self.dvec(l, s, 1 if which == 1 else 4)
        for kc in range(KC):
            if kc % 2 == 0:
                self.act(hT[:, kc, :N], tmp[:, kc, :N], ACT.Identity, reads=[btmp, self.b_v1], writes=[bhT],
                         scale=A[:, kc:kc + 1], bias=B[:, kc:kc + 1])
            else:
                self.ts("dve", hT[:, kc, :N], tmp[:, kc, :N], A[:, kc:kc + 1], B[:, kc:kc + 1], ALU.mult, ALU.add,
                        reads=[btmp, self.b_v1], writes=[bhT])

    def residual(self, y3, by, x3, bx, N, l, s, which, sq, bsq, rstd, brstd, tmp, btmp, out3, bout):
        self.rstd_of(y3, [by], KC, N, D, sq, bsq, rstd, brstd)
        self.tt("dve", tmp[:, :, :N], y3, rstd[:, :N].unsqueeze(1).to_broadcast([128, KC, N]), ALU.mult,
                reads=[by, brstd], writes=[btmp])
        G = self.dvec(l, s, 2 if which == 1 else 5)
        for kc in range(KC):
            self.stt("dve", out3[:, kc, :N], tmp[:, kc, :N], G[:, kc:kc + 1], x3[:, kc, :N], ALU.mult, ALU.add,
                     reads=[btmp, bx, self.b_v1], writes=[bout])

    def step_tail(self):
        if self.pending is not None:
            try:
                next(self.pending)
            except StopIteration:
                self.pending = None

    def step_G(self):
        if self.pendingG is not None:
            try:
                next(self.pendingG)
            except StopIteration:
                self.pendingG = None

    def flush_G(self):
        while self.pendingG is not None:
            self.step_G()

    def flush_tail(self):
        while self.pending is not None:
            self.step_tail()

    def xbufs(self, c0, N):
        return [self.b_xTg[i] for i in range(c0 // 256, (c0 + N) // 256)]

    def xT_src(self, c0, N):
        return self.xT_d[:, :, c0:c0 + N].rearrange("k p n -> p k n")

    def phase_A(self):
        P = self.P
        w = self.ab(KC * 3072).rearrange("p (k n) -> p k n", k=KC); b_w = Buf()
        stage = [self.af(3072), self.af(3072)]; b_stage = [Buf(), Buf()]
        self.load_w(w, self.w_in0, KC, 3072, stage, b_stage, b_w, 3072)
        xtok = [self.af(1024), self.af(1024)]; b_xtok = [Buf(), Buf()]
        xT2 = [self.af(KC * 512).rearrange("p (k n) -> p k n", k=KC) for _ in range(2)]; b_x2 = [Buf(), Buf()]
        sq = self.ab(KC * 512).rearrange("p (k n) -> p k n", k=KC); b_sq = Buf()
        rstd = self.af(512); b_rstd = Buf()
        tmp = self.af(KC * 512).rearrange("p (k n) -> p k n", k=KC); b_tmp = Buf()
        hT2 = [self.ab(KC * 512).rearrange("p (k n) -> p k n", k=KC) for _ in range(2)]; b_h2 = [Buf(), Buf()]
        qs = self.ab(4 * 512).rearrange("p (k n) -> p k n", k=4); b_qsst = Buf()
        gs = self.ab(4 * 512).rearrange("p (k n) -> p k n", k=4); b_gsst = Buf()
        zs = self.af(8 * 512).rearrange("p (k n) -> p k n", k=8); b_zst = Buf()
        uv = self.ab(4 * 1024).rearrange("p (t n) -> p t n", t=4); b_uv = Buf()
        blocks = [(SEQ, CTX, 1)] + [(i * 512, 512, 0) for i in range(SEQ // 512)]
        self.tcount = 0

        def stage_T(bi):
            c0, N, s = blocks[bi]
            NT = N // 128
            xT, b_x = xT2[bi % 2], b_x2[bi % 2]
            hT, b_h = hT2[bi % 2], b_h2[bi % 2]
            for ti in range(NT):
                xt, bxt = xtok[self.tcount % 2], b_xtok[self.tcount % 2]
                self.tcount += 1
                src = self.ctx_in[ti * 128:(ti + 1) * 128, :] if s else self.x_in[c0 + ti * 128:c0 + (ti + 1) * 128, :]
                self.ld(xt, src, writes=[bxt])
                for half in range(2):
                    ps, bp = self.nextbank(4, 8)
                    for k in range(4):
                        kc = half * 4 + k
                        P.op("pe", lambda h, o_=ps[:, k * 128:(k + 1) * 128], i_=xt[:, kc * 128:(kc + 1) * 128]: h.transpose(o_, i_, self.ident_f),
                             reads=[bxt, self.b_c], writes=bp)
                    self.cp("act" if half == 0 else "dve", xT[:, half * 4:(half + 1) * 4, ti * 128:(ti + 1) * 128],
                            ps[:, 0:512].rearrange("p (k n) -> p k n", k=4), reads=bp, writes=[b_x])
            self.st(self.xT_src(c0, N), xT[:, :, :N], reads=[b_x], writes=self.xbufs(c0, N))
            self.norm_mod(xT[:, :, :N], b_x, N, 0, s, 1, sq, b_sq, rstd, b_rstd, tmp, b_tmp, hT, b_h)

        def stage_P(bi):
            c0, N, s = blocks[bi]
            NT = N // 128
            hT, b_h = hT2[bi % 2], b_h2[bi % 2]
            for j in range(16):
                kind, hh = divmod(j, 4)
                col = [512, 1024, 1536, 2560][kind] + hh * 128
                ps, bp = self.nextbank(0, 4)
                for kc in range(KC):
                    self.mm(ps[:, :N], w[:, kc, col:col + 128], hT[:, kc, :N], kc == 0, kc == KC - 1, reads=[b_w, b_h], writes=bp)
                if kind == 0:
                    self.act(qs[:, hh, :N], ps[:, :N], ACT.Silu, reads=bp, writes=[b_qsst])
                elif kind == 3:
                    self.act(gs[:, hh, :N], ps[:, :N], ACT.Silu, reads=bp, writes=[b_gsst])
                else:
                    self.cp("dve", zs[:, (kind - 1) * 4 + hh, :N], ps[:, :N], reads=bp, writes=[b_zst])
            for ti in range(NT):
                for wi, col in enumerate((0, 2048)):
                    ps, bp = self.nextbank(0, 4)
                    for kc in range(KC):
                        self.mm(ps[:, :], hT[:, kc, ti * 128:(ti + 1) * 128], w[:, kc, col:col + 512], kc == 0, kc == KC - 1, reads=[b_w, b_h], writes=bp)
                    self.cp("act" if wi == 0 else "dve", uv[:, ti, wi * 512:(wi + 1) * 512], ps[:, :], reads=bp, writes=[b_uv])
            self.st(self.qs_d[:, :, c0:c0 + N].rearrange("k p n -> p k n"), qs[:, :, :N], reads=[b_qsst], writes=[self.b_qs])
            self.st(self.gs_d[:, :, c0:c0 + N].rearrange("k p n -> p k n"), gs[:, :, :N], reads=[b_gsst], writes=[self.b_gs])
            self.st(self.z_d[:, :, c0:c0 + N].rearrange("k p n -> p k n"), zs[:, :, :N], reads=[b_zst], writes=[self.b_z])
            self.st(self.u_d[c0:c0 + N, :].rearrange("(t p) e -> p t e", p=128), uv[:, :NT, 0:512], reads=[b_uv], writes=[self.b_u])
            self.st(self.v_d[c0:c0 + N, :].rearrange("(t p) e -> p t e", p=128), uv[:, :NT, 512:1024], reads=[b_uv], writes=[self.b_v])

        stage_T(0)
        for bi in range(len(blocks)):
            if bi + 1 < len(blocks):
                stage_T(bi + 1)
            stage_P(bi)

    def phase_B(self):
        P = self.P
        wo = self.ab(KC * D).rearrange("p (k n) -> p k n", k=KC); b_wo = Buf()
        pw = self.ab(4 * 128).rearrange("p (k n) -> p k n", k=4); b_pw = Buf()
        pm = self.ab(36 * 128).rearrange("p (k n) -> p k n", k=36); b_pm = Buf()
        invc = self.af(12 * 128).rearrange("p (k n) -> p k n", k=12); b_invc = Buf()
        p_stage = self.ptr
        stage = [self.af(4608), self.af(4608)]; b_stage = [Buf(), Buf()]
        self.load_w(wo, self.w_out0, KC, D, stage, b_stage, b_wo, D)
        for g in range(4):
            self.ld(stage[0][:, 0:128], self.pool_w[g], writes=[b_stage[0]])
            self.cp("dve", pw[:, g, :], stage[0][:, 0:128], reads=[b_stage[0]], writes=[b_pw])
        self.ld(stage[1][:, 0:4608], self.pm_in, writes=[b_stage[1]])
        self.cp("dve", pm, stage[1][:, 0:4608].rearrange("p (k n) -> p k n", k=36), reads=[b_stage[1]], writes=[b_pm])
        self.ld(invc, self.invc_in.rearrange("p (k n) -> p k n", k=12), writes=[b_invc])
        self.P.barrier()
        self.ptr = p_stage
        S2 = [self.af(512).rearrange("p (h e) -> p h e", h=4) for _ in range(2)]; b_S2 = [Buf(), Buf()]
        Sbf2 = [self.ab(512).rearrange("p (h e) -> p h e", h=4) for _ in range(2)]; b_Sbf2 = [Buf(), Buf()]
        qsb = self.ab(4 * 512).rearrange("p (k n) -> p k n", k=4); b_qsb = Buf()
        zb = self.af(4 * 512).rearrange("p (k n) -> p k n", k=4); b_zb = Buf()
        vc = self.ab(8 * 512).rearrange("p (c e) -> p c e", c=8); b_vc = Buf()
        sg = self.af(4 * 512).rearrange("p (k n) -> p k n", k=4); b_sg = Buf()
        lf = self.af(4 * 512).rearrange("p (k n) -> p k n", k=4); b_lf = Buf()
        kk = self.af(4 * 512).rearrange("p (k n) -> p k n", k=4); b_kk = Buf()
        bb = self.af(4 * 512).rearrange("p (k n) -> p k n", k=4); b_bb = Buf()
        qt2 = [self.ab(4 * 512).rearrange("p (k n) -> p k n", k=4) for _ in range(2)]; b_qt2 = [Buf(), Buf()]
        kt2 = [self.ab(4 * 512).rearrange("p (k n) -> p k n", k=4) for _ in range(2)]; b_kt2 = [Buf(), Buf()]
        ebl2 = [self.af(32).rearrange("p (h c) -> p h c", h=4) for _ in range(2)]; b_ebl2 = [Buf(), Buf()]
        ktok = [self.ab(512), self.ab(512)]; b_ktok = [Buf(), Buf()]
        Abf = [self.ab(256).rearrange("p (h t) -> p h t", h=4) for _ in range(3)]; b_Abf = [Buf(), Buf(), Buf()]
        ost2 = [self.af(4 * 512).rearrange("p (k n) -> p k n", k=4) for _ in range(2)]; b_ost2 = [Buf(), Buf()]
        gsb = self.ab(4 * 512).rearrange("p (k n) -> p k n", k=4); b_gsb = Buf()
        ub = self.ab(6 * 512).rearrange("p (t e) -> p t e", t=6); b_ub = Buf()
        xT = self.af(KC * 512).rearrange("p (k n) -> p k n", k=KC); b_x = Buf()
        sq = self.ab(KC * 512).rearrange("p (k n) -> p k n", k=KC); b_sq = Buf()

        mixT = self.ab(KC * 512).rearrange("p (k n) -> p k n", k=KC); b_mix = Buf()
        pooled = self.ab(4 * 512).rearrange("p (k n) -> p k n", k=4); b_pooled = Buf()
        yT = self.af(KC * 512).rearrange("p (k n) -> p k n", k=KC); b_y = Buf()
        rr4 = yT[:, 0:4, :]; b_rr4 = b_y
        rstd = self.af(512); b_rstd = Buf()
        lbv = self.hv[:, 0:8]; oml = self.hv[:, 8:16]; noml = self.hv[:, 16:24]

        for dirn in range(2):
            P.op("pool", lambda h: h.memset(S2[0], 0.0), writes=[b_S2[0]])
            self.sgen = 1
            P.op("pool", lambda h: h.memset(Sbf2[0], 0.0), writes=[b_Sbf2[0]])
            lat = [(i * 512, 512, 0) for i in range(SEQ // 512)]
            blocks = [(SEQ, CTX, 1)] + (lat if dirn == 0 else lat[::-1])
            def tail_gen(bi, c0, N, s, NT, seg0, segL, ost, b_ost):
                self.ld(gsb[:, :, :N], self.gs_d[:, :, c0:c0 + N].rearrange("k p n -> p k n"), reads=[self.b_gs], writes=[b_gsb])
                self.ld(xT[:, :, :N], self.xT_src(c0, N), reads=self.xbufs(c0, N), writes=[b_x])
                lo = max(c0 - 128, seg0); hi = min(c0 + N + 128, seg0 + segL)
                slot0 = 1 - (c0 - lo) // 128
                nld = (hi - lo) // 128
                self.ld(ub[:, slot0:slot0 + nld, :], self.u_d[lo:hi, :].rearrange("(t p) e -> p t e", p=128), reads=[self.b_u], writes=[b_ub])
                self.act(sq[:, 0:4, :N], ost[:, :, :N], ACT.Square, reads=[b_ost], writes=[b_sq])
                for hh in range(4):
                    ps, bp = self.tailbank()
                    self.mm(ps[:, :N], self.ones_b, sq[:, hh, :N], True, True, reads=[b_sq, self.b_c], writes=bp)
                    self.act(rr4[:, hh, :N], ps[:, :N], ACT.Ln, reads=bp + [self.b_c], writes=[b_rr4], scale=1.0 / 128, bias=self.eps_t[:, 0:1])
                self.act(rr4[:, :, :N], rr4[:, :, :N], ACT.Exp, reads=[b_rr4], writes=[b_rr4], scale=-0.5)
                self.tt("dve", ost[:, :, :N], ost[:, :, :N], rr4[:, :, :N], ALU.mult, reads=[b_ost, b_rr4], writes=[b_ost])
                self.stt("dve", mixT[:, 4:8, :N], ost[:, :, :N], self.vT2[:, 84:85], gsb[:, :, :N], ALU.mult, ALU.mult,
                         reads=[b_ost, b_gsb, self.b_v1], writes=[b_mix])
                yield
                ntiles_seg = segL // 128
                for g in range(4):
                    psp, bpp = self.tailbank()
                    ttps = []
                    for ti in range(NT):
                        gt = (c0 - seg0) // 128 + ti
                        ttp = 0 if gt == 0 else (2 if gt == ntiles_seg - 1 else 1)
                        ttps.append(ttp)
                        rels = [r for r in (-1, 0, 1) if 0 <= gt + r < ntiles_seg]
                        for ri, r in enumerate(rels):
                            self.mm(psp[:, ti * 128:(ti + 1) * 128], ub[:, 1 + ti + r, g * 128:(g + 1) * 128], pm[:, (ttp * 4 + g) * 3 + (r + 1), :],
                                    ri == 0, ri == len(rels) - 1, reads=[b_ub, b_pm], writes=bpp)
                    if all(t_ == 1 for t_ in ttps):
                        self.tt("dve", pooled[:, g, :N].rearrange("p (t n) -> p t n", t=NT), psp[:, :N].rearrange("p (t n) -> p t n", t=NT),
                                invc[:, 4 + g, :].unsqueeze(1).to_broadcast([128, NT, 128]), ALU.mult, reads=bpp + [b_invc], writes=[b_pooled])
                    else:
                        for ti in range(NT):
                            self.tt("dve", pooled[:, g, ti * 128:(ti + 1) * 128], psp[:, ti * 128:(ti + 1) * 128], invc[:, ttps[ti] * 4 + g, :], ALU.mult,
                                    reads=bpp + [b_invc], writes=[b_pooled])
                    psy, bpy = self.tailbank()
                    self.mm(psy[:, :N], pw[:, g, :], pooled[:, g, :N], True, True, reads=[b_pw, b_pooled], writes=bpy)
                    self.act(mixT[:, g, :N], psy[:, :N], ACT.Identity, reads=bpy + [self.b_v1], writes=[b_mix], scale=self.vT2[:, 64 + g:65 + g])
                    yield
                for oc in range(KC):
                    ps, bp = self.tailbank()
                    for kc in range(KC):
                        self.mm(ps[:, :N], wo[:, kc, oc * 128:(oc + 1) * 128], mixT[:, kc, :N], kc == 0, kc == KC - 1, reads=[b_wo, b_mix], writes=bp)
                    self.cp("act" if oc % 2 == 0 else "dve", yT[:, oc, :N], ps[:, :N], reads=bp, writes=[b_y])
                    if oc % 2 == 1:
                        yield
                self.in_tail = True
                self.residual(yT[:, :, :N], b_y, xT, b_x, N, 0, s, 1, sq, b_sq, rstd, b_rstd, yT, b_y, yT, b_y)
                self.in_tail = False
                self.st(self.xT_src(c0, N), yT[:, :, :N], reads=[b_y], writes=self.xbufs(c0, N))

            def G(bi):
                c0, N, s = blocks[bi]
                qt, b_qt = qt2[bi % 2], b_qt2[bi % 2]
                kt, b_kt = kt2[bi % 2], b_kt2[bi % 2]
                ebl, b_ebl = ebl2[bi % 2], b_ebl2[bi % 2]
                NCH = N // 64
                NT = N // 128
                seg0, segL = (SEQ, CTX) if s else (0, SEQ)
                self.ld(qsb[:, :, :N], self.qs_d[:, :, c0:c0 + N].rearrange("k p n -> p k n"), reads=[self.b_qs], writes=[b_qsb])
                self.ld(zb[:, :, :N], self.z_d[dirn * 4:(dirn + 1) * 4, :, c0:c0 + N].rearrange("k p n -> p k n"), reads=[self.b_z], writes=[b_zb])
                for hh in range(4):
                    self.act(sg[:, hh, :N], zb[:, hh, :N], ACT.Sigmoid, reads=[b_zb], writes=[b_sg])
                yield
                for hh in range(4):
                    i8 = dirn * 4 + hh
                    self.act(lf[:, hh, :N], sg[:, hh, :N], ACT.Ln, reads=[b_sg, self.b_v1], writes=[b_lf],
                             scale=oml[:, i8:i8 + 1], bias=lbv[:, i8:i8 + 1])
                    self.ts("pool", kk[:, hh, :N], sg[:, hh, :N], noml[:, i8:i8 + 1], oml[:, i8:i8 + 1], ALU.mult, ALU.add,
                            reads=[b_sg, self.b_v1], writes=[b_kk])
                yield
                for hh in range(4):
                    P.op("dve", lambda h, o_=bb[:, hh, :N], d0=self.cmask[:, :N], d1=lf[:, hh, :N]: h.tensor_tensor_scan(
                        out=o_, data0=d0, data1=d1, initial=0.0, op0=ALU.mult, op1=ALU.add),
                         reads=[b_lf, self.b_c], writes=[b_bb])
                    if dirn == 1:
                        b3 = bb[:, hh, :N].rearrange("p (c k) -> p c k", k=64)
                        l3 = lf[:, hh, :N].rearrange("p (c k) -> p c k", k=64)
                        t3 = sg[:, hh, :N].rearrange("p (c k) -> p c k", k=64)
                        self.tt("dve", t3, b3[:, :, 63:64].to_broadcast([128, NCH, 64]), b3, ALU.subtract, reads=[b_bb], writes=[b_sg])
                        self.tt("dve", b3, t3, l3, ALU.add, reads=[b_sg, b_lf], writes=[b_bb])
                yield
                for hh in range(4):
                    self.act(lf[:, hh, :N], bb[:, hh, :N], ACT.Exp, reads=[b_bb], writes=[b_lf])
                    self.act(sg[:, hh, :N], bb[:, hh, :N], ACT.Exp, reads=[b_bb], writes=[b_sg], scale=-1.0)
                yield
                self.tt("dve", qt[:, :, :N], qsb[:, :, :N], lf[:, :, :N], ALU.mult, reads=[b_qsb, b_lf], writes=[b_qt])
                self.tt("dve", kt[:, :, :N], kk[:, :, :N], sg[:, :, :N], ALU.mult, reads=[b_kk, b_sg], writes=[b_kt])
                epos = 63 if dirn == 0 else 0
                self.cp("pool", ebl[:, :, :NCH], lf[:, :, :N].rearrange("p h (c k) -> p h c k", k=64)[:, :, :, epos], reads=[b_lf], writes=[b_ebl])
            def C(bi):
                ost, b_ost = ost2[bi % 2], b_ost2[bi % 2]
                c0, N, s = blocks[bi]
                qt, b_qt = qt2[bi % 2], b_qt2[bi % 2]
                kt, b_kt = kt2[bi % 2], b_kt2[bi % 2]
                ebl, b_ebl = ebl2[bi % 2], b_ebl2[bi % 2]
                NCH = N // 64
                NT = N // 128
                seg0, segL = (SEQ, CTX) if s else (0, SEQ)
                self.ld(vc[0:64, :NCH, :], self.v_d[c0:c0 + N, :].rearrange("(c p) e -> p c e", p=64), reads=[self.b_v], writes=[b_vc])
                order = list(range(NCH)) if dirn == 0 else list(range(NCH))[::-1]
                if dirn == 1:
                    self.ld(ost[:, :, :N], self.o1_d[:, :, c0:c0 + N].rearrange("k p n -> p k n"), reads=[self.b_o1], writes=[b_ost])
                def stage1(ci):
                    c = order[ci]
                    cs = slice(c * 64, (c + 1) * 64)
                    d = ci % 2
                    a3 = ci % 3
                    psTb, bT = self.bank(0 + d)
                    psT = psTb[0:64, 0:256].bitcast(BF16)
                    psAb, bA = self.bank(2 + d)
                    psA = psAb[0:64, 0:256]
                    for hh in range(4):
                        P.op("pe", lambda h, o_=psT[:, hh * 128:(hh + 1) * 128], i_=kt[:, hh, cs]: h.transpose(o_, i_, self.ident_b),
                             reads=[b_kt, self.b_c], writes=bT)
                    self.cp("act", ktok[d][0:64, :], psT, reads=bT, writes=[b_ktok[d]])
                    for hh in range(4):
                        self.mm(psA[:, hh * 64:(hh + 1) * 64], kt[:, hh, cs], qt[:, hh, cs], True, True, reads=[b_kt, b_qt], writes=bA)
                    self.tt("dve", Abf[a3][0:64, :, :], psA.rearrange("p (h t) -> p h t", h=4),
                            self.mask_f[dirn].unsqueeze(1).to_broadcast([64, 4, 64]), ALU.mult, reads=bA + [self.b_c], writes=[b_Abf[a3]])

                def stage2(ci):
                    c = order[ci]
                    d = ci % 2
                    psKV, bKV = self.bank(4 + d)
                    for hh in range(4):
                        self.mm(psKV[:, hh * 128:(hh + 1) * 128], ktok[d][0:64, hh * 128:(hh + 1) * 128], vc[0:64, c, hh * 128:(hh + 1) * 128], True, True,
                                reads=[b_ktok[d], b_vc], writes=bKV)

                def update(ci):
                    c = order[ci]
                    d = ci % 2
                    g_ = self.sgen % 2
                    self.sgen += 1
                    psKV, bKV = self.bank(4 + d)
                    So, bSo = S2[1 - g_], b_S2[1 - g_]
                    Sn, bSn = S2[g_], b_S2[g_]
                    self.tt("dve", Sn, So, psKV[:, :].rearrange("p (h e) -> p h e", h=4), ALU.add, reads=bKV + [bSo], writes=[bSn])
                    self.tt("dve", Sn, Sn, ebl[:, :, c:c + 1].to_broadcast([128, 4, 128]), ALU.mult, reads=[bSn, b_ebl], writes=[bSn])
                    self.cp("act", Sbf2[g_], Sn, reads=[bSn], writes=[b_Sbf2[g_]])

                def back(ci):
                    c = order[ci]
                    cs = slice(c * 64, (c + 1) * 64)
                    d = ci % 2
                    gp = (self.sgen - 2) % 2 if True else 0
                    psOc, bOc = self.bank(6 + d)
                    for hh in range(4):
                        oc_ = psOc[:, hh * 64:(hh + 1) * 64]
                        self.mm(oc_, vc[0:64, c, hh * 128:(hh + 1) * 128], Abf[ci % 3][0:64, hh, :], True, False, reads=[b_vc, b_Abf[ci % 3]], writes=bOc)
                        self.mm(oc_, Sbf2[gp][:, hh, :], qt[:, hh, cs], False, True, reads=[b_Sbf2[gp], b_qt], writes=bOc)
                    o3 = psOc[:, 0:256].rearrange("p (h t) -> p h t", h=4)
                    if dirn == 0:
                        self.cp("act", ost[:, :, cs], o3, reads=bOc, writes=[b_ost])
                    else:
                        self.tt("dve", ost[:, :, cs], o3, ost[:, :, cs], ALU.add, reads=bOc + [b_ost], writes=[b_ost])

                stage1(0)
                stage1(1)
                stage2(0)
                for ci in range(NCH):
                    if ci + 2 < NCH:
                        stage1(ci + 2)
                    if ci + 1 < NCH:
                        stage2(ci + 1)
                    update(ci)
                    self.step_G()
                    back(ci)
                    self.step_tail()
                self.flush_G()
                if dirn == 0:
                    self.st(self.o1_d[:, :, c0:c0 + N].rearrange("k p n -> p k n"), ost[:, :, :N], reads=[b_ost], writes=[self.b_o1])
                    return
                self.flush_tail()
                self.pending = tail_gen(bi, c0, N, s, NT, seg0, segL, ost, b_ost)

            self.pending = None
            self.pendingG = G(0)
            self.flush_G()
            for bi in range(len(blocks)):
                self.pendingG = G(bi + 1) if bi + 1 < len(blocks) else None
                C(bi)
            self.flush_tail()

    def phase_FFN(self, l, blocks, final):
        P = self.P
        w1 = self.ab(KC * 2 * FH).rearrange("p (k n) -> p k n", k=KC); b_w1 = Buf()
        w2 = self.ab(HC * D).rearrange("p (k n) -> p k n", k=HC); b_w2 = Buf()
        g_region = self.af(HC * 256)
        stage = [g_region[:, 0:2816], g_region[:, 2816:5632]]; b_stage = [Buf(), Buf()]
        self.load_w(w1, self.ffn_w_in[l], KC, 2 * FH, stage, b_stage, b_w1, 2816)
        self.load_w(w2, self.ffn_w_out[l], HC, D, stage, b_stage, b_w2, D)
        self.P.barrier()
        gT = g_region.bitcast(BF16).rearrange("p (k n) -> p k n", k=HC); b_g = Buf()
        fT = self.af(KC * 512).rearrange("p (k n) -> p k n", k=KC); b_f = Buf()
        rstd = self.af(512); b_rstd = Buf()
        tmpk = [self.af(512), self.af(512)]; b_tmpk = [Buf(), Buf()]
        xk = [self.af(512), self.af(512)]; b_xk = [Buf(), Buf()]
        sqk = [self.ab(512), self.ab(512)]; b_sqk = [Buf(), Buf()]
        hT2 = [self.ab(KC * 512).rearrange("p (k n) -> p k n", k=KC) for _ in range(2)]; b_h2 = [Buf(), Buf()]
        ga = self.af(512); b_ga = Buf()
        psS, bpS = self.ps[7], self.bps[7]

        def finish_rstd(N):
            self.act(rstd[:, :N], psS[:, :N], ACT.Ln, reads=bpS + [self.b_c], writes=[b_rstd], scale=1.0 / D, bias=self.eps_t[:, 0:1])
            self.act(rstd[:, :N], rstd[:, :N], ACT.Exp, reads=[b_rstd], writes=[b_rstd], scale=-0.5)

        def gen_N(bi):
            c0, N = blocks[bi]
            s = 1 if c0 >= SEQ else 0
            hT, b_h = hT2[bi % 2], b_h2[bi % 2]
            A = self.dvec(l, s, 3); B = self.dvec(l, s, 4)
            xb_ = self.xbufs(c0, N)
            for kc in range(KC + 1):
                if kc < KC:
                    self.ld(xk[kc % 2][:, :N], self.xT_d[kc, :, c0:c0 + N], reads=xb_, writes=[b_xk[kc % 2]])
                    self.act(sqk[kc % 2][:, :N], xk[kc % 2][:, :N], ACT.Square, reads=[b_xk[kc % 2]], writes=[b_sqk[kc % 2]])
                if kc >= 1:
                    j = kc - 1
                    self.mm(psS[:, :N], self.ones_b, sqk[j % 2][:, :N], j == 0, j == KC - 1, reads=[b_sqk[j % 2], self.b_c], writes=bpS)
                yield
            finish_rstd(N)
            yield
            for kc in range(KC + 1):
                if kc < KC:
                    self.ld(xk[kc % 2][:, :N], self.xT_d[kc, :, c0:c0 + N], reads=xb_, writes=[b_xk[kc % 2]])
                if kc >= 1:
                    j = kc - 1
                    t_, bt_ = tmpk[j % 2], b_tmpk[j % 2]
                    self.tt("dve", t_[:, :N], xk[j % 2][:, :N], rstd[:, :N], ALU.mult, reads=[b_xk[j % 2], b_rstd], writes=[bt_])
                    if j % 2 == 0:
                        self.act(hT[:, j, :N], t_[:, :N], ACT.Identity, reads=[bt_, self.b_v1], writes=[b_h], scale=A[:, j:j + 1], bias=B[:, j:j + 1])
                    else:
                        self.ts("dve", hT[:, j, :N], t_[:, :N], A[:, j:j + 1], B[:, j:j + 1], ALU.mult, ALU.add, reads=[bt_, self.b_v1], writes=[b_h])
                yield

        def gen_R(bi):
            c0, N = blocks[bi]
            s = 1 if c0 >= SEQ else 0
            G = self.dvec(l, s, 5)
            xb_ = self.xbufs(c0, N)
            for kc in range(KC + 1):
                if kc < KC:
                    self.act(sqk[kc % 2][:, :N], fT[:, kc, :N], ACT.Square, reads=[b_f], writes=[b_sqk[kc % 2]])
                if kc >= 1:
                    j = kc - 1
                    self.mm(psS[:, :N], self.ones_b, sqk[j % 2][:, :N], j == 0, j == KC - 1, reads=[b_sqk[j % 2], self.b_c], writes=bpS)
                if kc % 2 == 0:
                    yield
            finish_rstd(N)
            yield
            for kc in range(KC + 1):
                if kc < KC:
                    self.ld(xk[kc % 2][:, :N], self.xT_d[kc, :, c0:c0 + N], reads=xb_, writes=[b_xk[kc % 2]])
                if kc >= 1:
                    j = kc - 1
                    t_, bt_ = tmpk[j % 2], b_tmpk[j % 2]
                    self.tt("dve", t_[:, :N], fT[:, j, :N], rstd[:, :N], ALU.mult, reads=[b_f, b_rstd], writes=[bt_])
                    self.stt("dve", fT[:, j, :N], t_[:, :N], G[:, j:j + 1], xk[j % 2][:, :N], ALU.mult, ALU.add,
                             reads=[bt_, b_xk[j % 2], self.b_v1, b_f], writes=[b_f])
                yield
            if not final:
                self.st(self.xT_src(c0, N), fT[:, :, :N], reads=[b_f], writes=xb_)
            else:
                for ti in range(N // 128):
                    for half in range(2):
                        ps2, bp2 = self.nextbank(4, 7)
                        for k in range(4):
                            kc = half * 4 + k
                            P.op("pe", lambda h, o_=ps2[:, k * 128:(k + 1) * 128], i_=fT[:, kc, ti * 128:(ti + 1) * 128]: h.transpose(o_, i_, self.ident_f),
                                 reads=[b_f, self.b_c], writes=bp2)
                        o2, bo2 = tmpk[half], b_tmpk[half]
                        self.cp("act" if half == 0 else "dve", o2, ps2[:, :], reads=bp2, writes=[bo2])
                        r0 = c0 + ti * 128
                        self.st(self.out[r0:r0 + 128, half * 512:(half + 1) * 512], o2, reads=[bo2], writes=[])
                    yield

        def step(name):
            g_ = self.ffn_gen.get(name)
            if g_ is None:
                return False
            try:
                next(g_)
            except StopIteration:
                self.ffn_gen[name] = None
            return True

        def flush(name):
            while self.ffn_gen.get(name) is not None:
                step(name)

        self.ffn_gen = {"N": gen_N(0), "R": None}
        flush("N")
        for bi, (c0, N) in enumerate(blocks):
            hT, b_h = hT2[bi % 2], b_h2[bi % 2]
            self.ffn_gen["N"] = gen_N(bi + 1) if bi + 1 < len(blocks) else None
            for hc in range(HC):
                psa, bpa = self.nextbank(0, 4)
                psb, bpb = self.nextbank(0, 4)
                for kc in range(KC):
                    self.mm(psa[:, :N], w1[:, kc, hc * 128:(hc + 1) * 128], hT[:, kc, :N], kc == 0, kc == KC - 1, reads=[b_w1, b_h], writes=bpa)
                for kc in range(KC):
                    self.mm(psb[:, :N], w1[:, kc, FH + hc * 128:FH + (hc + 1) * 128], hT[:, kc, :N], kc == 0, kc == KC - 1, reads=[b_w1, b_h], writes=bpb)
                self.act(ga[:, :N], psa[:, :N], ACT.Silu, reads=bpa, writes=[b_ga])
                self.tt("dve", gT[:, hc, :N], ga[:, :N], psb[:, :N], ALU.mult, reads=[b_ga] + bpb, writes=[b_g])
                if self.ffn_gen["R"] is not None:
                    step("R")
                else:
                    step("N")
            flush("R")
            for oc in range(KC):
                ps, bp = self.nextbank(4, 7)
                for hc in range(HC):
                    self.mm(ps[:, :N], w2[:, hc, oc * 128:(oc + 1) * 128], gT[:, hc, :N], hc == 0, hc == HC - 1, reads=[b_w2, b_g], writes=bp)
                self.cp("act" if oc % 2 == 0 else "dve", fT[:, oc, :N], ps[:, :N], reads=bp, writes=[b_f])
                step("N")
                step("N")
                step("N")
            flush("N")
            self.ffn_gen["R"] = gen_R(bi)
        flush("R")

    def phase_DE(self):
        P = self.P
        NKT = LT // 128
        KT = self.ab(2 * LT).rearrange("p (k n) -> p k n", k=2); b_KT = Buf()
        V = self.ab(NKT * 256).rearrange("p (t e) -> p t e", t=NKT); b_V = Buf()
        keepDE = self.ptr
        w = self.ab(KC * 1536).rearrange("p (k n) -> p k n", k=KC); b_w = Buf()
        stage = [self.af(1536), self.af(1536)]; b_stage = [Buf(), Buf()]
        self.load_w(w, self.att_w_in, KC, 1536, stage, b_stage, b_w, 1536)
        rstd = self.af(512); b_rstd = Buf()
        tmpk = [self.af(512), self.af(512)]; b_tmpk = [Buf(), Buf()]
        xk = [self.af(512), self.af(512)]; b_xk = [Buf(), Buf()]
        sqk = [self.ab(512), self.ab(512)]; b_sqk = [Buf(), Buf()]
        hT2 = [self.ab(KC * 512).rearrange("p (k n) -> p k n", k=KC) for _ in range(2)]; b_h2 = [Buf(), Buf()]
        psS, bpS = self.ps[7], self.bps[7]

        def gen_N(bi):
            c0, N, s = blocks[bi]
            hT_, b_h_ = hT2[bi % 2], b_h2[bi % 2]
            A = self.dvec(1, s, 0); B = self.dvec(1, s, 1)
            xb_ = self.xbufs(c0, N)
            for kc in range(KC + 1):
                if kc < KC:
                    self.ld(xk[kc % 2][:, :N], self.xT_d[kc, :, c0:c0 + N], reads=xb_, writes=[b_xk[kc % 2]])
                    self.act(sqk[kc % 2][:, :N], xk[kc % 2][:, :N], ACT.Square, reads=[b_xk[kc % 2]], writes=[b_sqk[kc % 2]])
                if kc >= 1:
                    j = kc - 1
                    self.mm(psS[:, :N], self.ones_b, sqk[j % 2][:, :N], j == 0, j == KC - 1, reads=[b_sqk[j % 2], self.b_c], writes=bpS)
                yield
            self.act(rstd[:, :N], psS[:, :N], ACT.Ln, reads=bpS + [self.b_c], writes=[b_rstd], scale=1.0 / D, bias=self.eps_t[:, 0:1])
            self.act(rstd[:, :N], rstd[:, :N], ACT.Exp, reads=[b_rstd], writes=[b_rstd], scale=-0.5)
            yield
            for kc in range(KC + 1):
                if kc < KC:
                    self.ld(xk[kc % 2][:, :N], self.xT_d[kc, :, c0:c0 + N], reads=xb_, writes=[b_xk[kc % 2]])
                if kc >= 1:
                    j = kc - 1
                    t_, bt_ = tmpk[j % 2], b_tmpk[j % 2]
                    self.tt("dve", t_[:, :N], xk[j % 2][:, :N], rstd[:, :N], ALU.mult, reads=[b_xk[j % 2], b_rstd], writes=[bt_])
                    if j % 2 == 0:
                        self.act(hT_[:, j, :N], t_[:, :N], ACT.Identity, reads=[bt_, self.b_v1], writes=[b_h_], scale=A[:, j:j + 1], bias=B[:, j:j + 1])
                    else:
                        self.ts("dve", hT_[:, j, :N], t_[:, :N], A[:, j:j + 1], B[:, j:j + 1], ALU.mult, ALU.add, reads=[bt_, self.b_v1], writes=[b_h_])
                yield

        def stepN(k=1):
            for _ in range(k):
                if self.pendN is not None:
                    try:
                        next(self.pendN)
                    except StopIteration:
                        self.pendN = None

        def flushN():
            while self.pendN is not None:
                stepN()

        cosb = self.af(512); sinb = self.af(512); b_cs = Buf()
        mk3 = lambda: [self.af(512) for _ in range(3)]
        raw = mk3(); b_raw = [Buf() for _ in range(3)]
        sq1 = [self.ab(512) for _ in range(3)]; b_sq1 = [Buf() for _ in range(3)]
        rs1 = mk3(); b_rs1 = [Buf() for _ in range(3)]
        kn = mk3(); b_kn = [Buf() for _ in range(3)]
        t1 = mk3(); b_t1 = [Buf() for _ in range(3)]
        t2 = mk3(); b_t2 = [Buf() for _ in range(3)]
        qst = self.ab(8 * 512).rearrange("p (k n) -> p k n", k=8); b_qst = Buf()
        blocks = [(SEQ, CTX, 1)] + [(i * 512, 512, 0) for i in range(SEQ // 512)]
        self.jbase = 0
        self.pendN = gen_N(0)
        flushN()
        for bi, (c0, N, s) in enumerate(blocks):
            NT = N // 128
            hT, b_h = hT2[bi % 2], b_h2[bi % 2]
            self.pendN = gen_N(bi + 1) if bi + 1 < len(blocks) else None
            if not s:
                self.ld(cosb, self.cos_in[:, c0:c0 + N], writes=[b_cs])
                self.ld(sinb, self.sin_in[:, c0:c0 + N], writes=[b_cs])
            heads = [("k", kv) for kv in range(2)]
            if (not s) and c0 < OWN:
                heads += [("q", hq) for hq in range(8)]

            def hinfo(j):
                kind, hx = heads[j]
                d = (self.jbase + j) % 3
                col = (1024 + hx * 128) if kind == "k" else hx * 128
                gcol = 86 if kind == "k" else 85
                dst = KT[:, hx, c0:c0 + N] if kind == "k" else qst[:, hx, :N]
                bdst = b_KT if kind == "k" else b_qst
                return d, col, gcol, dst, bdst

            def s1(j):
                d, col, gcol, dst, bdst = hinfo(j)
                ps, bp = self.nextbank(0, 4)
                for kc in range(KC):
                    self.mm(ps[:, :N], w[:, kc, col:col + 128], hT[:, kc, :N], kc == 0, kc == KC - 1, reads=[b_w, b_h], writes=bp)
                self.cp("dve", raw[d][:, :N], ps[:, :N], reads=bp, writes=[b_raw[d]])
                self.act(sq1[d][:, :N], raw[d][:, :N], ACT.Square, reads=[b_raw[d]], writes=[b_sq1[d]])

            def s2(j):
                d, col, gcol, dst, bdst = hinfo(j)
                ps2, bp2 = self.nextbank(4, 6)
                self.mm(ps2[:, :N], self.ones_b, sq1[d][:, :N], True, True, reads=[b_sq1[d], self.b_c], writes=bp2)
                self.act(rs1[d][:, :N], ps2[:, :N], ACT.Ln, reads=bp2 + [self.b_c], writes=[b_rs1[d]], scale=1.0 / 128, bias=self.eps_t[:, 0:1])
                self.act(rs1[d][:, :N], rs1[d][:, :N], ACT.Exp, reads=[b_rs1[d]], writes=[b_rs1[d]], scale=-0.5)
                if s:
                    self.stt("dve", dst, raw[d][:, :N], self.vT2[:, gcol:gcol + 1], rs1[d][:, :N], ALU.mult, ALU.mult,
                             reads=[b_raw[d], b_rs1[d], self.b_v1], writes=[bdst])
                else:
                    self.stt("dve", kn[d][:, :N], raw[d][:, :N], self.vT2[:, gcol:gcol + 1], rs1[d][:, :N], ALU.mult, ALU.mult,
                             reads=[b_raw[d], b_rs1[d], self.b_v1], writes=[b_kn[d]])

            def s3(j):
                d, col, gcol, dst, bdst = hinfo(j)
                if s:
                    return
                ps3, bp3 = self.nextbank(6, 7)
                self.mm(ps3[:, :N], self.rotT, kn[d][:, :N], True, True, reads=[b_kn[d], self.b_c], writes=bp3)
                self.tt("pool", t1[d][:, :N], kn[d][:, :N], cosb[:, :N], ALU.mult, reads=[b_kn[d], b_cs], writes=[b_t1[d]])
                self.tt("dve", t2[d][:, :N], ps3[:, :N], sinb[:, :N], ALU.mult, reads=bp3 + [b_cs], writes=[b_t2[d]])
                self.tt("pool", dst, t1[d][:, :N], t2[d][:, :N], ALU.add, reads=[b_t1[d], b_t2[d]], writes=[bdst])

            nj = len(heads)
            for i in range(nj + 2):
                if i < nj:
                    s1(i)
                if 0 <= i - 1 < nj:
                    s2(i - 1)
                if 0 <= i - 2 < nj:
                    s3(i - 2)
                stepN(2)
            self.jbase += nj
            if (not s) and c0 < OWN:
                self.st(self.q1_d[:, :, c0:c0 + N].rearrange("k p n -> p k n"), qst[:, :, :N], reads=[b_qst], writes=[self.b_q1])
            for ti in range(NT):
                ps, bp = self.nextbank(0, 4)
                for kc in range(KC):
                    self.mm(ps[:, 0:256], hT[:, kc, ti * 128:(ti + 1) * 128], w[:, kc, 1280:1536], kc == 0, kc == KC - 1, reads=[b_w, b_h], writes=bp)
                self.cp("act", V[:, (c0 // 128) + ti, :], ps[:, 0:256], reads=bp, writes=[b_V])
                stepN(2)
            flushN()
        self.P.barrier()
        self.ptr = keepDE
        wo = self.ab(KC * D).rearrange("p (k n) -> p k n", k=KC); b_wo = Buf()
        p_st = self.ptr
        stage = [self.af(1024), self.af(1024)]; b_stage = [Buf(), Buf()]
        self.load_w(wo, self.att_w_out, KC, D, stage, b_stage, b_wo, D)
        self.P.barrier()
        self.ptr = p_st
        qb = self.ab(8 * 512).rearrange("p (k n) -> p k n", k=8); b_qb = Buf()
        xT = self.af(KC * 512).rearrange("p (k n) -> p k n", k=KC); b_x = Buf()
        pt = [self.ab(1024) for _ in range(4)]; b_pt = [Buf() for _ in range(4)]
        acc = self.af(1024); b_acc = Buf()
        accp = self.af(1024); b_accp = Buf()
        ones_f = self.af(128); b_of = Buf()
        P.op("pool", lambda h: h.memset(ones_f, 1.0), writes=[b_of])
        rs = self.af(512); b_rs = Buf()
        attnT = self.ab(8 * 512).rearrange("p (k n) -> p k n", k=8); b_at = Buf()
        yT = self.af(KC * 512).rearrange("p (k n) -> p k n", k=KC); b_y = Buf()
        sq = self.ab(KC * 512).rearrange("p (k n) -> p k n", k=KC); b_sq = Buf()
        rstd = self.af(512); b_rstd = Buf()
        xo = self.af(KC * 512).rearrange("p (k n) -> p k n", k=KC); b_xo = Buf()
        scale = 128 ** -0.5
        N = 512
        NKP = NKT // 2
        for qi in range(OWN // 512):
            c0 = qi * 512
            self.ld(qb, self.q1_d[:, :, c0:c0 + N].rearrange("k p n -> p k n"), reads=[self.b_q1], writes=[b_qb])
            self.ld(xT, self.xT_src(c0, N), reads=self.xbufs(c0, N), writes=[b_x])
            seq = [(hq, kp) for hq in range(8) for kp in range(NKP)]

            def issue_S(idx):
                hq, kp = seq[idx]
                kv = hq // 4
                b0 = (idx % 3) * 2
                for j in range(2):
                    kt_ = 2 * kp + j
                    self.mm(self.ps[b0 + j][:, :], KT[:, kv, kt_ * 128:(kt_ + 1) * 128], qb[:, hq, :], True, True,
                            reads=[b_KT, b_qb], writes=self.bps[b0 + j])
                self.act(pt[idx % 4], self.psall[:, b0 * 512:(b0 + 2) * 512], ACT.Exp, reads=self.bps[b0] + self.bps[b0 + 1],
                         writes=[b_pt[idx % 4]], scale=scale)

            issue_S(0)
            issue_S(1)
            for idx, (hq, kp) in enumerate(seq):
                kv = hq // 4
                psO, bO = self.bank(6)
                psL, bL = self.bank(7)
                if idx + 2 < len(seq):
                    issue_S(idx + 2)
                p_, bp_ = pt[idx % 4], b_pt[idx % 4]
                for j in range(2):
                    kt_ = 2 * kp + j
                    self.mm(psO[:, :], V[:, kt_, kv * 128:(kv + 1) * 128], p_[:, j * 512:(j + 1) * 512], kt_ == 0, kt_ == NKT - 1,
                            reads=[b_V, bp_], writes=bO)
                if kp == 0:
                    self.cp("dve", acc, p_, reads=[bp_], writes=[b_acc])
                elif kp % 4 == 3:
                    self.mm(psL[:, :], self.ones_b, p_[:, 0:512], kp == 3, False, reads=[self.b_c, bp_], writes=bL)
                    self.mm(psL[:, :], self.ones_b, p_[:, 512:1024], False, False, reads=[self.b_c, bp_], writes=bL)
                else:
                    self.tt("dve", acc, acc, p_, ALU.add, reads=[bp_, b_acc], writes=[b_acc])
                if kp == NKP - 1:
                    self.mm(psL[:, :], ones_f, acc[:, 0:512], False, False, reads=[b_of, b_acc], writes=bL)
                    self.mm(psL[:, :], ones_f, acc[:, 512:1024], False, True, reads=[b_of, b_acc], writes=bL)
                    self.act(rs, psL[:, :], ACT.Ln, reads=bL, writes=[b_rs])
                    self.act(rs, rs, ACT.Exp, reads=[b_rs], writes=[b_rs], scale=-1.0)
                    self.tt("dve", attnT[:, hq, :], psO[:, :], rs, ALU.mult, reads=bO + [b_rs], writes=[b_at])
            for oc in range(KC):
                ps, bp = self.nextbank(0, 4)
                for kc in range(KC):
                    self.mm(ps[:, :], wo[:, kc, oc * 128:(oc + 1) * 128], attnT[:, kc, :], kc == 0, kc == KC - 1, reads=[b_wo, b_at], writes=bp)
                self.cp("act" if oc % 2 == 0 else "dve", yT[:, oc, :], ps[:, :], reads=bp, writes=[b_y])
            self.residual(yT, b_y, xT, b_x, N, 1, 0, 1, sq, b_sq, rstd, b_rstd, yT, b_y, xo, b_xo)
            self.st(self.xT_src(c0, N), xo, reads=[b_xo], writes=self.xbufs(c0, N))


def _pool_consts(mirror):
    L = 384
    wins = (2, 4, 8, 16)
    pm = np.zeros((3, 4, 3, 128, 128), np.float32)
    invc = np.zeros((3, 4, 128), np.float32)
    for tt in range(3):
        for g, w in enumerate(wins):
            for t in range(128):
                T = tt * 128 + t
                lo = T - w // 2 + (1 if mirror else 0)
                hi = lo + w
                lo = max(lo, 0); hi = min(hi, L)
                cnt = hi - lo
                invc[tt, g, t] = 1.0 / cnt
                for S in range(lo, hi):
                    r = S // 128 - tt
                    pm[tt, g, r + 1, S % 128, t] += 1.0
                pm[tt, g, 1, t, t] -= cnt
    pm_l = np.ascontiguousarray(pm.transpose(3, 0, 1, 2, 4).reshape(128, 36 * 128))
    invc_l = np.ascontiguousarray(np.broadcast_to(invc.reshape(1, 12 * 128), (128, 12 * 128)))
    return pm_l, invc_l


def _rope_tables(flip):
    j = np.arange(SEQ)
    t = (SEQ - 1 - j) if flip else j
    row = (t // 64).astype(np.float32)
    colp = (t % 64).astype(np.float32)
    inv = (np.float32(10000.0) ** (-(np.arange(32, dtype=np.float32) / np.float32(32)))).astype(np.float32)
    cosT = np.zeros((128, SEQ), np.float32); sinT = np.zeros((128, SEQ), np.float32)
    for p in range(128):
        pos = row if p < 64 else colp
        ang = (pos * inv[p % 32]).astype(np.float32)
        cosT[p] = np.cos(ang); sinT[p] = np.sin(ang)
    return cosT, sinT


def _consts():
    c = np.zeros((128, 896), np.float32)
    c[:, 0:128] = np.eye(128, dtype=np.float32)
    R = np.zeros((128, 128), np.float32)
    for p in range(128):
        if (p // 32) % 2 == 0:
            R[p, p + 32] = -1.0
        else:
            R[p, p - 32] = 1.0
    c[:, 128:256] = R.T
    s = np.arange(64)[:, None]; t = np.arange(64)[None, :]
    c[0:64, 256:320] = (s <= t).astype(np.float32)
    c[0:64, 320:384] = (s >= t).astype(np.float32)
    m = np.ones(512, np.float32); m[::64] = 0.0
    c[:, 384:896] = m[None, :]
    return c


_CACHE = {}


def _get_nc(dbg=None, stop_after=None):
    key = (tuple(dbg) if dbg else None, stop_after)
    if key not in _CACHE:
        _CACHE[key] = Builder(dbg=dbg, stop_after=stop_after).build()
    return _CACHE[key]


def make_in_maps(x, c, ctx, c_ctx, ada_w, ada_b, norm_g, ab_w_in, ab_w_out, pool_w, pool_scale,
                 hg_lower, hg_onorm_g, att_w_in, att_w_out, att_qnorm_g, att_knorm_g, ffn_w_in, ffn_w_out):
    f = lambda a: np.ascontiguousarray(np.asarray(a, dtype=np.float32))
    x = f(x); c = f(c); ctx = f(ctx); c_ctx = f(c_ctx); ada_w = f(ada_w); ada_b = f(ada_b); norm_g = f(norm_g)
    ab_w_in = f(ab_w_in); ab_w_out = f(ab_w_out); pool_w = f(pool_w); pool_scale = f(pool_scale); hg_lower = f(hg_lower)
    consts = _consts()
    shared = {
        "ada_w": ada_w, "w_out0": f(ab_w_out[0]), "pool_w": f(pool_w[0]), "att_w_in": f(att_w_in[0]),
        "att_w_out": f(att_w_out[0]), "ffn_w_in": f(ffn_w_in), "ffn_w_out": f(ffn_w_out), "consts": consts,
    }
    per_h = []
    for h in range(2):
        w_in0 = ab_w_in[0].copy()
        hl = hg_lower.copy()
        if h == 1:
            w_in0[:, 1024:1536] = ab_w_in[0][:, 1536:2048]
            w_in0[:, 1536:2048] = ab_w_in[0][:, 1024:1536]
            hl = hl[:, ::-1, :]
        pm_l, invc_l = _pool_consts(mirror=(h == 1))
        cosT, sinT = _rope_tables(flip=(h == 1))
        v2 = np.zeros((87, 128), np.float32)
        v2[0:64] = norm_g.reshape(64, 128)
        v2[64:68] = pool_scale[0].reshape(4, 128)
        v2[68:76] = hl[0].reshape(8, 128)
        v2[76:84] = hl[1].reshape(8, 128)
        v2[84] = f(hg_onorm_g)[0]
        v2[85] = f(att_qnorm_g)[0]
        v2[86] = f(att_knorm_g)[0]
        per_h.append({"w_in0": np.ascontiguousarray(w_in0), "pmats": pm_l, "invcnt": invc_l, "cosT": cosT, "sinT": sinT, "vecs2": v2})
    in_maps = []
    for core in range(NCORES):
        b, h = divmod(core, 2)
        xl = x[b] if h == 0 else x[b, ::-1]
        cl = ctx[b] if h == 0 else ctx[b, ::-1]
        v1 = np.zeros((112, 128), np.float32)
        v1[0:8] = c[b].reshape(8, 128)
        v1[8:16] = c_ctx.reshape(8, 128)
        v1[16:112] = ada_b.reshape(96, 128)
        m = dict(shared)
        m.update(per_h[h])
        m["x_loc"] = np.ascontiguousarray(xl)
        m["ctx_loc"] = np.ascontiguousarray(cl)
        m["vecs1"] = v1
        in_maps.append(m)
    return in_maps


def kernel(**inputs):
    nc = _get_nc()
    in_maps = make_in_maps(**inputs)
    res = run_bass_kernel_spmd(nc, in_maps, core_ids=list(range(NCORES)))
    out = np.zeros((4, SEQ, D), np.float32)
    for core in range(NCORES):
        b, h = divmod(core, 2)
        o = np.asarray(res.results[core]["out_loc"], dtype=np.float32)
        if h == 0:
            out[b, 0:OWN] = o
        else:
            out[b, OWN:SEQ] = o[::-1]
    return out
```
